# Optimizing a Trainium2 kernel written in Bass

```python
import math
import jax, jax.numpy as jnp
from jax import lax
import numpy as np

D_MODEL = 4096
BATCH = 4
SEQ = 2048
DEPTH = 2

CHUNK = 64
Q_BLOCK = 128
MIX_WIDTH = D_MODEL
N_GROUPS = 4
GROUP_WIDTH = MIX_WIDTH // N_GROUPS
HEAD_DIM = 128
N_HEADS = GROUP_WIDTH // HEAD_DIM
DIFF_HALF = HEAD_DIM // 2
GDN_CONV = 4
MLA_Q_RANK = 3 * D_MODEL // 16
MLA_KV_RANK = D_MODEL // 16
MLA_NOPE = 128
MLA_ROPE = 64
MLA_V = HEAD_DIM
ROPE_THETA = 10000.0
BAND_CHUNKS = 9
BAND = BAND_CHUNKS * CHUNK
REL_CLIP = 128
DEEPNORM_ALPHA = (2 * DEPTH) ** 0.25
DEEPNORM_BETA = (8 * DEPTH) ** -0.25
IN_SIZES = (GROUP_WIDTH, GROUP_WIDTH, GROUP_WIDTH,
            3 * GROUP_WIDTH, N_HEADS, N_HEADS,
            MLA_Q_RANK, MLA_KV_RANK + MLA_ROPE,
            GROUP_WIDTH, GROUP_WIDTH, GROUP_WIDTH,
            MIX_WIDTH)
IN_COLS = sum(IN_SIZES)

kernel_name = 'hymba_style_streaming_hybrid_encoder'


def _rms_norm(x, g, eps=1e-6):
    xf = x.astype(jnp.float32)
    y = xf * lax.rsqrt(jnp.mean(xf * xf, axis=-1, keepdims=True) + eps)
    return (y * g.astype(jnp.float32)).astype(x.dtype)


def _layer_norm(x, g, b, eps=1e-5):
    xf = x.astype(jnp.float32)
    xc = xf - jnp.mean(xf, axis=-1, keepdims=True)
    var = jnp.mean(xc * xc, axis=-1, keepdims=True)
    return (xc * lax.rsqrt(var + eps) * g.astype(jnp.float32) + b.astype(jnp.float32)).astype(x.dtype)


def _l2_normalize(x, eps=1e-6):
    return x * lax.rsqrt(jnp.sum(x * x, axis=-1, keepdims=True) + eps)


def _split_cols(h, sizes):
    out, start = [], 0
    for n in sizes:
        out.append(h[..., start:start + n])
        start += n
    return out


def _chunk_visible(qpos, kpos):
    return (kpos[None, :] // CHUNK) <= (qpos[:, None] // CHUNK)


def _sweep_query_blocks(q, block_fn):
    b, t = q.shape[:2]
    nqb = t // Q_BLOCK
    qb = jnp.moveaxis(q.reshape((b, nqb, Q_BLOCK) + q.shape[2:]), 1, 0)
    out = lax.map(lambda a: block_fn(a[0], a[1] * Q_BLOCK + jnp.arange(Q_BLOCK)), (qb, jnp.arange(nqb)))
    out = jnp.moveaxis(out, 0, 1)
    return out.reshape((b, t) + out.shape[3:])


def _diff_attention(q, k, v, lam):
    b, t = q.shape[:2]
    kpos = jnp.arange(t)
    slopes = 2.0 ** (-8.0 * jnp.arange(1, N_HEADS + 1, dtype=jnp.float32) / N_HEADS)
    scale = DIFF_HALF ** -0.5

    def block(qi, qpos):
        s = jnp.einsum('bqmd,bkmd->bmqk', qi, k).astype(jnp.float32) * scale
        s = s.reshape(b, N_HEADS, 2, Q_BLOCK, t)
        dist = jnp.abs(qpos[:, None] - kpos[None, :]).astype(jnp.float32)
        s = s - slopes[:, None, None, None] * dist
        s = jnp.where(_chunk_visible(qpos, kpos), s, -jnp.inf)
        p = jax.nn.softmax(s, axis=-1)
        w = p[:, :, 0] - lam * p[:, :, 1]
        return jnp.einsum('bhqk,bkhd->bqhd', w.astype(v.dtype), v)

    return _sweep_query_blocks(q, block)


def _causal_depthwise_conv(x, w):
    kw = w.shape[0]
    return lax.conv_general_dilated(x, w[:, None, :], window_strides=(1,), padding=[(kw - 1, 0)],
                                    dimension_numbers=('NWC', 'WIO', 'NWC'),
                                    feature_group_count=x.shape[-1])


def _gated_delta_net(qkv, a, bt, conv_w, a_log, dt_bias, norm_w):
    out_dtype = qkv.dtype
    f32 = jnp.float32
    b, t, _ = qkv.shape
    nc = t // CHUNK
    qkv = jax.nn.silu(_causal_depthwise_conv(qkv, conv_w)).astype(f32)
    q, k, v = jnp.split(qkv, 3, axis=-1)

    def to_chunks(z):
        z = z.reshape((b, nc, CHUNK) + z.shape[2:])
        return jnp.moveaxis(z, 3, 1)

    q = to_chunks(_l2_normalize(q.reshape(b, t, N_HEADS, HEAD_DIM)) * HEAD_DIM ** -0.5)
    k = to_chunks(_l2_normalize(k.reshape(b, t, N_HEADS, HEAD_DIM)))
    v = to_chunks(v.reshape(b, t, N_HEADS, HEAD_DIM))
    beta = to_chunks(jax.nn.sigmoid(bt.astype(f32)))
    g = -jnp.exp(a_log.astype(f32)) * jax.nn.softplus(a.astype(f32) + dt_bias.astype(f32))
    g = jnp.cumsum(to_chunks(g), axis=-1)

    lower = jnp.tril(jnp.ones((CHUNK, CHUNK), dtype=bool))
    strict = jnp.tril(jnp.ones((CHUNK, CHUNK), dtype=bool), -1)
    gdiff = g[..., :, None] - g[..., None, :]
    decay = jnp.where(lower, jnp.exp(jnp.where(lower, gdiff, 0.0)), 0.0)
    k_beta = k * beta[..., None]
    kk = jnp.einsum('bhncd,bhnsd->bhncs', k_beta, k) * decay
    tri = jnp.where(strict, kk, 0.0) + jnp.eye(CHUNK, dtype=f32)
    u = lax.linalg.triangular_solve(tri, v * beta[..., None], left_side=True, lower=True, unit_diagonal=True)
    w = lax.linalg.triangular_solve(tri, k_beta * jnp.exp(g)[..., None], left_side=True, lower=True,
                                    unit_diagonal=True)
    qk = jnp.where(lower, jnp.einsum('bhncd,bhnsd->bhncs', q, k) * decay, 0.0)

    def step(state, inp):
        q_c, k_c, u_c, w_c, g_c, qk_c = inp
        v_new = u_c - jnp.einsum('bhck,bhkv->bhcv', w_c, state)
        o_c = (jnp.einsum('bhck,bhkv->bhcv', q_c * jnp.exp(g_c)[..., None], state)
               + jnp.einsum('bhcs,bhsv->bhcv', qk_c, v_new))
        g_last = g_c[..., -1:]
        state = (state * jnp.exp(g_last)[..., None]
                 + jnp.einsum('bhck,bhcv->bhkv', k_c * jnp.exp(g_last - g_c)[..., None], v_new))
        return state, o_c

    xs = [jnp.moveaxis(z, 2, 0) for z in (q, k, u, w, g, qk)]
    state0 = jnp.zeros((b, N_HEADS, HEAD_DIM, HEAD_DIM), f32)
    _, o = lax.scan(step, state0, xs)
    o = jnp.moveaxis(jnp.moveaxis(o, 0, 2), 1, 3).reshape(b, t, N_HEADS, HEAD_DIM)
    return _rms_norm(o, norm_w).astype(out_dtype)


def _rope(x, pos):
    half = MLA_ROPE // 2
    inv = ROPE_THETA ** (-jnp.arange(half, dtype=jnp.float32) / half)
    ang = pos.astype(jnp.float32)[:, None] * inv[None, :]
    cos = jnp.cos(ang)[None, :, None, :]
    sin = jnp.sin(ang)[None, :, None, :]
    xf = x.astype(jnp.float32)
    x1, x2 = xf[..., :half], xf[..., half:]
    return jnp.concatenate([x1 * cos - x2 * sin, x2 * cos + x1 * sin], axis=-1).astype(x.dtype)


def _mla(c_dq, c_dkv, q_norm, w_uq, kv_norm, w_ukv):
    b, t = c_dq.shape[:2]
    pos = jnp.arange(t)
    q = (_rms_norm(c_dq, q_norm) @ w_uq).reshape(b, t, N_HEADS, MLA_NOPE + MLA_ROPE)
    q = jnp.concatenate([q[..., :MLA_NOPE], _rope(q[..., MLA_NOPE:], pos)], axis=-1)
    kv = (_rms_norm(c_dkv[..., :MLA_KV_RANK], kv_norm) @ w_ukv).reshape(b, t, N_HEADS, MLA_NOPE + MLA_V)
    k_rope = _rope(c_dkv[..., None, MLA_KV_RANK:], pos)
    k = jnp.concatenate([kv[..., :MLA_NOPE], jnp.broadcast_to(k_rope, (b, t, N_HEADS, MLA_ROPE))], axis=-1)
    v = kv[..., MLA_NOPE:]
    kpos = jnp.arange(t)
    scale = (MLA_NOPE + MLA_ROPE) ** -0.5

    def block(qi, qpos):
        s = jnp.einsum('bqhd,bkhd->bhqk', qi, k).astype(jnp.float32) * scale
        s = jnp.where(_chunk_visible(qpos, kpos), s, -jnp.inf)
        p = jax.nn.softmax(s, axis=-1)
        return jnp.einsum('bhqk,bkhd->bqhd', p.astype(v.dtype), v)

    return _sweep_query_blocks(q, block)


def _band_attention(q, k, v, rel_bias):
    b, t = q.shape[:2]
    nc = t // CHUNK
    pad = (BAND_CHUNKS - 1) * CHUNK
    kp = jnp.pad(k, ((0, 0), (pad, 0), (0, 0), (0, 0)))
    vp = jnp.pad(v, ((0, 0), (pad, 0), (0, 0), (0, 0)))
    band_idx = jnp.arange(BAND)
    rel = jnp.arange(CHUNK)[:, None] + pad - band_idx[None, :]
    bias = rel_bias.astype(jnp.float32)[:, jnp.clip(rel, -REL_CLIP, REL_CLIP) + REL_CLIP]
    qc = jnp.moveaxis(q.reshape(b, nc, CHUNK, N_HEADS, HEAD_DIM), 1, 0)
    scale = HEAD_DIM ** -0.5

    def chunk_fn(args):
        qi, c = args
        start = c * CHUNK
        kb = lax.dynamic_slice_in_dim(kp, start, BAND, axis=1)
        vb = lax.dynamic_slice_in_dim(vp, start, BAND, axis=1)
        s = jnp.einsum('bqhd,bkhd->bhqk', qi, kb).astype(jnp.float32) * scale + bias
        s = jnp.where(start + band_idx >= pad, s, -jnp.inf)
        p = jax.nn.softmax(s, axis=-1)
        return jnp.einsum('bhqk,bkhd->bqhd', p.astype(vb.dtype), vb)

    out = lax.map(chunk_fn, (qc, jnp.arange(nc)))
    return jnp.moveaxis(out, 0, 1).reshape(b, t, N_HEADS, HEAD_DIM)


def _hybrid_layer(x, layer_idx, w_in, diff_lambda, diff_norm, gdn_conv, gdn_a_log, gdn_dt_bias, gdn_norm,
                  mla_q_norm, mla_w_uq, mla_kv_norm, mla_w_ukv, rel_bias, w_out, ln_gain, ln_bias):
    b, t, _ = x.shape
    h = jnp.einsum('btd,dc->btc', x, w_in)
    (a_q, a_k, a_v, b_qkv, b_a, b_b, c_dq, c_dkv, d_q, d_k, d_v, gate) = _split_cols(h, IN_SIZES)

    lam_init = 0.8 - 0.6 * math.exp(-0.3 * layer_idx)
    lam_p = diff_lambda.astype(jnp.float32)
    lam = jnp.exp(jnp.sum(lam_p[0] * lam_p[1])) - jnp.exp(jnp.sum(lam_p[2] * lam_p[3])) + lam_init
    o_a = _diff_attention(a_q.reshape(b, t, 2 * N_HEADS, DIFF_HALF), a_k.reshape(b, t, 2 * N_HEADS, DIFF_HALF),
                          a_v.reshape(b, t, N_HEADS, HEAD_DIM), lam)
    o_a = _rms_norm(o_a, diff_norm) * (1.0 - lam_init)

    o_b = _gated_delta_net(b_qkv, b_a, b_b, gdn_conv, gdn_a_log, gdn_dt_bias, gdn_norm)

    o_c = _mla(c_dq, c_dkv, mla_q_norm, mla_w_uq, mla_kv_norm, mla_w_ukv)

    o_d = _band_attention(d_q.reshape(b, t, N_HEADS, HEAD_DIM), d_k.reshape(b, t, N_HEADS, HEAD_DIM),
                          d_v.reshape(b, t, N_HEADS, HEAD_DIM), rel_bias)

    o = jnp.concatenate([o_a.reshape(b, t, GROUP_WIDTH), o_b.reshape(b, t, GROUP_WIDTH),
                         o_c.reshape(b, t, GROUP_WIDTH), o_d.reshape(b, t, GROUP_WIDTH)], axis=-1)
    o = o * jax.nn.silu(gate)
    y = jnp.einsum('btm,md->btd', o, w_out)
    return _layer_norm(DEEPNORM_ALPHA * x + y, ln_gain, ln_bias)


def setup_inputs(seed: int = 0) -> dict:
    key = jax.random.key(seed)
    ks = jax.random.split(key, 16)
    f32 = jnp.float32
    x = jax.random.normal(ks[0], (BATCH, SEQ, D_MODEL), f32)
    ones = lambda n: jnp.ones((n,), f32)
    vals = lambda n: jnp.full((n,), DEEPNORM_BETA, f32)
    col_scale = jnp.concatenate([ones(2 * GROUP_WIDTH), vals(GROUP_WIDTH),
                                 ones(2 * GROUP_WIDTH), vals(GROUP_WIDTH),
                                 ones(2 * N_HEADS + MLA_Q_RANK + MLA_KV_RANK + MLA_ROPE + 2 * GROUP_WIDTH),
                                 vals(GROUP_WIDTH), ones(MIX_WIDTH)])
    w_in = jax.random.normal(ks[1], (DEPTH, D_MODEL, IN_COLS), f32) * (D_MODEL ** -0.5) * col_scale
    diff_lambda = 0.1 * jax.random.normal(ks[2], (DEPTH, 4, DIFF_HALF), f32)
    diff_norm = 1.0 + 0.02 * jax.random.normal(ks[3], (DEPTH, HEAD_DIM), f32)
    gdn_conv = jax.random.normal(ks[4], (DEPTH, GDN_CONV, 3 * GROUP_WIDTH), f32) * (GDN_CONV ** -0.5)
    gdn_a_log = jnp.log(jax.random.uniform(ks[5], (DEPTH, N_HEADS), f32, 1.0, 16.0))
    dt = jnp.exp(jax.random.uniform(ks[6], (DEPTH, N_HEADS), f32, math.log(1e-3), math.log(1e-1)))
    gdn_dt_bias = dt + jnp.log(-jnp.expm1(-dt))
    gdn_norm = 1.0 + 0.02 * jax.random.normal(ks[7], (DEPTH, HEAD_DIM), f32)
    mla_q_norm = 1.0 + 0.02 * jax.random.normal(ks[8], (DEPTH, MLA_Q_RANK), f32)
    mla_w_uq = jax.random.normal(ks[9], (DEPTH, MLA_Q_RANK, N_HEADS * (MLA_NOPE + MLA_ROPE)), f32) * (MLA_Q_RANK ** -0.5)
    mla_kv_norm = 1.0 + 0.02 * jax.random.normal(ks[10], (DEPTH, MLA_KV_RANK), f32)
    ukv_scale = jnp.tile(jnp.concatenate([ones(MLA_NOPE), vals(MLA_V)]), N_HEADS)
    mla_w_ukv = (jax.random.normal(ks[11], (DEPTH, MLA_KV_RANK, N_HEADS * (MLA_NOPE + MLA_V)), f32)
                 * (MLA_KV_RANK ** -0.5) * ukv_scale)
    rel_bias = 0.5 * jax.random.normal(ks[12], (DEPTH, N_HEADS, 2 * REL_CLIP + 1), f32)
    w_out = jax.random.normal(ks[13], (DEPTH, MIX_WIDTH, D_MODEL), f32) * (MIX_WIDTH ** -0.5) * DEEPNORM_BETA
    ln_gain = 1.0 + 0.02 * jax.random.normal(ks[14], (DEPTH, D_MODEL), f32)
    ln_bias = 0.02 * jax.random.normal(ks[15], (DEPTH, D_MODEL), f32)
    return {'x': x, 'w_in': w_in, 'diff_lambda': diff_lambda, 'diff_norm': diff_norm,
            'gdn_conv': gdn_conv, 'gdn_a_log': gdn_a_log, 'gdn_dt_bias': gdn_dt_bias, 'gdn_norm': gdn_norm,
            'mla_q_norm': mla_q_norm, 'mla_w_uq': mla_w_uq, 'mla_kv_norm': mla_kv_norm, 'mla_w_ukv': mla_w_ukv,
            'rel_bias': rel_bias, 'w_out': w_out, 'ln_gain': ln_gain, 'ln_bias': ln_bias}


def reference(x, w_in, diff_lambda, diff_norm, gdn_conv, gdn_a_log, gdn_dt_bias, gdn_norm,
              mla_q_norm, mla_w_uq, mla_kv_norm, mla_w_ukv, rel_bias, w_out, ln_gain, ln_bias):
    for l in range(DEPTH):
        x = _hybrid_layer(x, l, w_in[l], diff_lambda[l], diff_norm[l], gdn_conv[l], gdn_a_log[l],
                          gdn_dt_bias[l], gdn_norm[l], mla_q_norm[l], mla_w_uq[l], mla_kv_norm[l],
                          mla_w_ukv[l], rel_bias[l], w_out[l], ln_gain[l], ln_bias[l])
    return x
```

```python
import numpy as np
import concourse.bass as bass
import concourse.mybir as mybir

F32 = mybir.dt.float32
BF16 = mybir.dt.bfloat16
U8 = mybir.dt.uint8
AF = mybir.ActivationFunctionType
ALU = mybir.AluOpType
AX = mybir.AxisListType

STREAMS = ("pe", "act", "dve", "pool", "sp")
NDMA_SEM = 8


class T:
    __slots__ = ("ap", "last_w", "readers", "rc")

    def __init__(self, ap):
        self.ap = ap
        self.last_w = None
        self.readers = []
        self.rc = {}

    def __getitem__(self, k):
        return self.ap[k]


class Sub:
    psum = True

    def __init__(self, par, ap):
        self.par = par
        self.ap = ap

    last_w = property(lambda s: s.par.last_w, lambda s, v: setattr(s.par, "last_w", v))
    readers = property(lambda s: s.par.readers, lambda s, v: setattr(s.par, "readers", v))
    rc = property(lambda s: s.par.rc, lambda s, v: setattr(s.par, "rc", v))


class Op:
    __slots__ = ("stream", "fn", "idx", "dma", "deps", "signaled", "rank", "dma_j")

    def __init__(self, stream, fn, idx, dma):
        self.stream = stream
        self.fn = fn
        self.idx = idx
        self.dma = dma
        self.deps = []
        self.signaled = False
        self.rank = 0
        self.dma_j = -1


class Prog:
    def __init__(self, nc, same_engine_sync=True):
        self.nc = nc
        self.ops = {s: [] for s in STREAMS}
        self.ndma = {s: 0 for s in STREAMS}
        self.dma_ops = {s: [] for s in STREAMS}
        self.fence = {s: None for s in STREAMS}
        self.same_engine_sync = same_engine_sync
        self.final_dmas = []

    def op(self, stream, fn, reads=(), writes=(), dma=False, final=False):
        o = Op(stream, fn, len(self.ops[stream]), dma)
        deps = []
        pr = [t for t in reads if getattr(t, "psum", False)]
        if pr:
            reads = [t for t in reads if not getattr(t, "psum", False)]
            writes = list(writes) + [t for t in pr if t not in writes]
        for t in reads:
            if t.last_w is not None:
                deps.append(t.last_w)
        for t in writes:
            if t.last_w is not None:
                deps.append(t.last_w)
            deps.extend(t.readers)
            deps.extend(t.rc.values())
        if self.fence[stream] is not None:
            deps.extend(self.fence[stream])
            self.fence[stream] = None
        seen = set()
        for d in deps:
            if d is o or id(d) in seen:
                continue
            seen.add(id(d))
            if (not d.dma) and d.stream == stream:
                if stream == "pe" or not self.same_engine_sync:
                    continue
            o.deps.append(d)
        for t in reads:
            if dma:
                t.readers.append(o)
            else:
                t.rc[stream] = o
        for t in writes:
            t.last_w = o
            t.readers = []
            t.rc = {}
        if dma:
            o.dma_j = self.ndma[stream]
            self.ndma[stream] += 1
            self.dma_ops[stream].append(o)
            if final:
                self.final_dmas.append(o)
        self.ops[stream].append(o)
        return o

    def barrier(self):
        deps = []
        for s in STREAMS:
            if self.ops[s]:
                for o in reversed(self.ops[s]):
                    if not o.dma:
                        deps.append(o)
                        break
            deps.extend(self.dma_ops[s][-NDMA_SEM:])
        for s in STREAMS:
            self.fence[s] = list(deps) + (self.fence[s] or [])

    def emit(self):
        nc = self.nc
        if self.final_dmas:
            fo = Op("sp", None, len(self.ops["sp"]), False)
            fo.deps = list(self.final_dmas)
            self.ops["sp"].append(fo)
        for s in STREAMS:
            for o in self.ops[s]:
                for d in o.deps:
                    if not d.dma:
                        d.signaled = True
        for s in STREAMS:
            r = 0
            for o in self.ops[s]:
                if o.signaled and not o.dma:
                    r += 1
                    o.rank = r
        from contextlib import ExitStack
        with ExitStack() as es:
            esem = {s: es.enter_context(nc.semaphore("S_" + s)) for s in STREAMS}
            dsem = {s: [es.enter_context(nc.semaphore("D_%s_%d" % (s, i))) for i in range(NDMA_SEM)]
                    for s in STREAMS if self.ndma[s] > 0}
            block = es.enter_context(nc.Block())

            def run_stream(s, eng):
                waited = {}
                for o in self.ops[s]:
                    need = {}
                    for d in o.deps:
                        if d.dma:
                            sem = dsem[d.stream][d.dma_j % NDMA_SEM]
                            val = 16 * (d.dma_j // NDMA_SEM + 1)
                        else:
                            sem = esem[d.stream]
                            val = d.rank
                        k = id(sem)
                        if k not in need or need[k][1] < val:
                            need[k] = (sem, val)
                    if o.dma and o.dma_j >= NDMA_SEM:
                        sem = dsem[s][o.dma_j % NDMA_SEM]
                        val = 16 * (o.dma_j // NDMA_SEM)
                        k = id(sem)
                        if k not in need or need[k][1] < val:
                            need[k] = (sem, val)
                    for k, (sem, val) in need.items():
                        if waited.get(k, 0) >= val:
                            continue
                        waited[k] = val
                        eng.wait_ge(sem, val)
                    if o.fn is None:
                        continue
                    ins = o.fn(eng)
                    if o.dma:
                        ins.then_inc(dsem[s][o.dma_j % NDMA_SEM], 16)
                    elif o.signaled:
                        ins.then_inc(esem[s], 1)

            @block.tensor
            def _(e):
                run_stream("pe", e)

            @block.scalar
            def _(e):
                run_stream("act", e)

            @block.vector
            def _(e):
                run_stream("dve", e)

            @block.gpsimd
            def _(e):
                run_stream("pool", e)

            @block.sync
            def _(e):
                run_stream("sp", e)

    def stats(self):
        return {s: len(self.ops[s]) for s in STREAMS}


class Arena:
    def __init__(self, base_ap, nbytes):
        self.base = base_ap
        self.nbytes = nbytes
        self.off = 0
        self.marks = []

    def alloc(self, cols, dtype, parts=128):
        esz = {F32: 4, BF16: 2, U8: 1}[dtype]
        n = cols * esz
        self.off = (self.off + 31) // 32 * 32
        assert self.off + n <= self.nbytes, ("SBUF arena overflow", self.off, n)
        v = self.base[0:parts, self.off:self.off + n]
        if dtype != U8:
            v = v.bitcast(dtype)
        self.off += n
        return v

    def tile(self, cols, dtype, parts=128):
        return T(self.alloc(cols, dtype, parts))

    def mark(self):
        self.marks.append(self.off)

    def release(self):
        self.off = self.marks.pop()
import math
from contextlib import ExitStack

D = 4096
KC = D // 128
ALPHA = 4.0 ** 0.25
LAM_INIT = [0.8 - 0.6 * math.exp(-0.3 * l) for l in range(2)]
NEG = -30000.0

O_AQ, O_AK, O_AV = 0, 1024, 2048
O_BQ, O_BK, O_BV, O_BA, O_BB = 3072, 4096, 5120, 6144, 6152
O_CQ, O_CKV, O_CKR = 6160, 6928, 7184
O_DQ, O_DK, O_DV, O_G = 7248, 8272, 9296, 10320


def fm_blocks(NH):
    bl = []
    for h in range(NH): bl.append(("Aq", h))
    for h in range(NH): bl.append(("Ak", h))
    for h in range(NH): bl.append(("Bq", h))
    for h in range(NH): bl.append(("Bk", h))
    for h in range(NH): bl.append(("Bv", h))
    for i in range(6): bl.append(("Ccq", i))
    for i in range(2): bl.append(("Cckv", i))
    bl.append(("Ckr", 0)); bl.append(("Ckr", 1))
    bl.append(("Bab", 0)); bl.append(("Bab", 1))
    for h in range(NH): bl.append(("Dq", h))
    for h in range(NH): bl.append(("Dk", h))
    for m in range(4):
        for h in range(NH): bl.append(("G", m * NH + h))
    assert len(bl) % 4 == 0
    return bl


class Builder:
    def __init__(self, nc, NH, T_, L, dbg=None, arena_kib=176):
        self.nc, self.NH, self.T, self.L = nc, NH, T_, L
        self.dbg = dbg or {}
        self.arena_kib = arena_kib
        self.NB = T_ // 128
        self.TP = min(1024, T_)
        self.fmb = fm_blocks(NH)
        self.NSF = len(self.fmb) // 4
        self.NST = 2 * (NH * 128 // 512)
        self.NS1 = self.NSF + self.NST

    def declare(self):
        nc, NH, T_, L = self.nc, self.NH, self.T, self.L
        di = lambda n, s, dt=F32: nc.dram_tensor(n, s, dt, kind="ExternalInput").ap()
        ds = lambda n, s, dt: nc.dram_tensor(n, s, dt, kind="Internal").ap()
        I = {}
        I["x"] = di("x", [T_, D])
        I["w1"] = di("w1", [L, self.NS1, 128, KC * 512])
        I["wout"] = di("wout", [L, 8, 128, 4 * NH * 512])
        I["wuq"] = di("wuq", [L, NH, 128, 6 * 256])
        I["wukvk"] = di("wukvk", [L, NH, 128, 2 * 128])
        I["wukvv"] = di("wukvv", [L, 128, 2 * NH * 128])
        I["qnorm"] = di("qnorm", [L, 128, 6])
        I["kvnorm"] = di("kvnorm", [L, 128, 2])
        I["dnorm"] = di("dnorm", [L, 128, 1])
        I["gnorm"] = di("gnorm", [L, 128, 1])
        I["dlam"] = di("dlam", [L, 1, 256])
        I["alog"] = di("alog", [L, NH, 1])
        I["dtb"] = di("dtb", [L, NH, 1])
        I["convw"] = di("convw", [L, 3 * NH, 128, 4])
        I["biasD"] = di("biasD", [L, NH, 128, 5 * 128])
        I["lng"] = di("lng", [L, 1, D])
        I["lnb"] = di("lnb", [L, 1, D])
        I["c_maskD"] = di("c_maskD", [128, 5 * 128])
        I["c_augK"] = di("c_augK", [NH, 3, T_])
        I["c_augQ"] = di("c_augQ", [NH, 3, T_])
        I["c_diagA"] = di("c_diagA", [NH, 128, 128])
        I["c_maskC"] = di("c_maskC", [128, 128])
        I["c_rope"] = di("c_rope", [2, 64, T_])
        I["c_sel"] = di("c_sel", [NH, NH * 128])
        I["c_tri"] = di("c_tri", [3, 128, 128])
        I["c_ident"] = di("c_ident", [128, 128])
        self.I = I
        S = {}
        for n in ("AqT", "AkT", "DqT", "DkT"):
            S[n] = ds("S_" + n, [NH, 128, T_], BF16)
        S["Av"] = ds("S_Av", [T_, NH * 128], BF16)
        S["Dv"] = ds("S_Dv", [T_, NH * 128], BF16)
        S["Bpre"] = ds("S_Bpre", [3 * NH, 128, T_], F32)
        S["Bab"] = ds("S_Bab", [2, 128, T_], F32)
        S["Ccq"] = ds("S_Ccq", [6, 128, T_], F32)
        S["Cckv"] = ds("S_Cckv", [2, 128, T_], F32)
        S["Ckr"] = ds("S_Ckr", [2, 128, T_], F32)
        S["gate"] = ds("S_gate", [4 * NH, 128, T_], BF16)
        if "oT" in self.dbg:
            S["oT"] = nc.dram_tensor("S_oT", [4 * NH, 128, T_], BF16, kind="ExternalOutput").ap()
        else:
            S["oT"] = ds("S_oT", [4 * NH, 128, T_], BF16)
        S["y"] = ds("S_y", [T_, D], F32)
        S["x1"] = ds("S_x1", [T_, D], F32)
        self.S = S
        self.out = nc.dram_tensor("out", [T_, D], F32, kind="ExternalOutput").ap()
        for n in self.dbg:
            if n in ("oT",):
                continue
            if n in S:
                pass

    def mm(self, ot, oap, lt, lap, rt, rap, start=True, stop=True, r=False):
        if r and self.dbg.get("f32r", False):
            lap = lap.bitcast(mybir.dt.float32r); rap = rap.bitcast(mybir.dt.float32r)
        self.P.op("pe", lambda e: e.matmul(oap, lhsT=lap, rhs=rap, start=start, stop=stop),
                  reads=[lt, rt], writes=[ot])

    def tr(self, ot, oap, it, iap, ident):
        self.P.op("pe", lambda e: e.transpose(oap, iap, ident.ap), reads=[it, ident], writes=[ot])

    def act(self, ot, oap, it, iap, func, bias=None, scale=1.0, bt=None, eng="act"):
        rd = [it] + ([bt] if bt is not None else [])
        if bias is None:
            self.P.op(eng, lambda e: e.activation(out=oap, in_=iap, func=func, scale=scale), reads=rd, writes=[ot])
        else:
            self.P.op(eng, lambda e: e.activation(out=oap, in_=iap, func=func, bias=bias, scale=scale), reads=rd, writes=[ot])

    def cp(self, eng, ot, oap, it, iap):
        if eng == "act":
            self.P.op("act", lambda e: e.copy(out=oap, in_=iap), reads=[it], writes=[ot])
        else:
            self.P.op(eng, lambda e: e.tensor_copy(out=oap, in_=iap), reads=[it], writes=[ot])

    def tt(self, ot, oap, at, aap, bt, bap, op, eng="dve"):
        self.P.op(eng, lambda e: e.tensor_tensor(out=oap, in0=aap, in1=bap, op=op), reads=[at, bt], writes=[ot])

    def ts(self, ot, oap, it, iap, s1, op0, s2=None, op1=None, st=None, eng="dve"):
        rd = [it] + ([st] if st is not None else [])
        if op1 is None:
            self.P.op(eng, lambda e: e.tensor_scalar(out=oap, in0=iap, scalar1=s1, scalar2=None, op0=op0), reads=rd, writes=[ot])
        else:
            self.P.op(eng, lambda e: e.tensor_scalar(out=oap, in0=iap, scalar1=s1, scalar2=s2, op0=op0, op1=op1), reads=rd, writes=[ot])

    def stt(self, ot, oap, at, aap, sc, bt, bap, op0, op1, st=None, eng="dve"):
        rd = [at, bt] + ([st] if st is not None else [])
        self.P.op(eng, lambda e: e.scalar_tensor_tensor(out=oap, in0=aap, scalar=sc, in1=bap, op0=op0, op1=op1),
                  reads=rd, writes=[ot])

    def rsum(self, ot, oap, it, iap, eng="dve"):
        self.P.op(eng, lambda e: e.reduce_sum(out=oap, in_=iap, axis=AX.X), reads=[it], writes=[ot])

    def recip(self, ot, oap, it, iap):
        self.P.op("dve", lambda e: e.reciprocal(out=oap, in_=iap), reads=[it], writes=[ot])

    def mset(self, ot, oap, val, eng="dve"):
        self.P.op(eng, lambda e: e.memset(oap, val), writes=[ot])

    def ld(self, ot, oap, src, q="sp", reads=()):
        return self.P.op(q, lambda e: e.dma_start(out=oap, in_=src), reads=list(reads), writes=[ot], dma=True)

    def st(self, dst, it, iap, q="sp", final=False):
        return self.P.op(q, lambda e: e.dma_start(out=dst, in_=iap), reads=[it], dma=True, final=final)

    def ld_cast(self, dst_t, dst_ap, src, stg, eng):
        n = dst_ap.shape[-1] if len(dst_ap.shape) == 2 else None
        sap = stg.ap[0:dst_ap.shape[0], 0:n]
        self.ld(stg, sap, src)
        self.cp(eng, dst_t, dst_ap, stg, sap)

    def rstd_col(self, out_t, in_t, scale, eps_t):
        self.act(out_t, out_t.ap, in_t, in_t.ap, AF.Sqrt, bias=eps_t.ap[0:out_t.ap.shape[0], :], scale=scale, bt=eps_t)
        self.recip(out_t, out_t.ap, out_t, out_t.ap)

    def build(self):
        nc = self.nc
        self.declare()
        with ExitStack() as es:
            nbytes = self.arena_kib * 1024
            sb = es.enter_context(nc.sbuf_tensor("arena", [128, nbytes], U8))
            self.banks = [es.enter_context(nc.psum_tensor("bank%d" % i, [128, 512], F32)) for i in range(8)]
            self.A = Arena(sb, nbytes)
            self.P = Prog(nc)
            A = self.A
            self.ident = A.tile(128, F32)
            self.identb = A.tile(128, BF16)
            self.ones = A.tile(128, F32)
            self.eps6 = A.tile(1, F32)
            self.eps5 = A.tile(1, F32)
            self.ld(self.ident, self.ident.ap, self.I["c_ident"])
            self.cp("dve", self.identb, self.identb.ap, self.ident, self.ident.ap)
            self.mset(self.ones, self.ones.ap, 1.0)
            self.mset(self.eps6, self.eps6.ap, 1e-6)
            self.mset(self.eps5, self.eps5.ap, 1e-5)
            phases = self.dbg.get("phases", "1ABCD3N")
            for l in range(self.L):
                xin = self.I["x"] if l == 0 else self.S["x1"]
                xout = self.out if l == self.L - 1 else self.S["x1"]
                if "1" in phases: self.phase1(l, xin)
                if "A" in phases: self.phaseA(l)
                if "C" in phases: self.phaseC(l)
                if "D" in phases: self.phaseD(l)
                if "B" in phases: self.phaseB(l)
                if "3" in phases: self.phase3(l)
                if "N" in phases: self.phaseN(l, xin, xout, final=(l == self.L - 1))
            self.P.emit()
        return nc

    def begin(self):
        self.P.barrier()
        self.A.mark()
        self.bk = [T(self.banks[i][:, :]) for i in range(8)]

    def end(self):
        if self.dbg.get("mem"):
            print("arena peak(approx cur) KiB:", self.A.off / 1024.0)
        self.A.release()
        self.P.barrier()

    def psum_tiles(self, bank_ids, cols, dtype=F32):
        out = []
        for b in bank_ids:
            for c0 in range(0, 512 - cols + 1, cols):
                ap = self.banks[b][:, c0:c0 + cols]
                if dtype == BF16:
                    ap = ap.bitcast(BF16)
                out.append(Sub(self.bk[b], ap))
        return out

    def fm_dest(self, kind, idx):
        S = self.S
        NH = self.NH
        if kind in ("Aq", "Ak", "Dq", "Dk"):
            return S[kind + "T"][idx], BF16, None
        if kind in ("Bq", "Bk", "Bv"):
            return S["Bpre"][{"Bq": 0, "Bk": 1, "Bv": 2}[kind] * NH + idx], F32, None
        if kind == "Ccq": return S["Ccq"][idx], F32, None
        if kind == "Cckv": return S["Cckv"][idx], F32, None
        if kind == "Ckr": return S["Ckr"][idx], F32, None
        if kind == "Bab": return S["Bab"][idx], F32, None
        if kind == "G": return S["gate"][idx], BF16, AF.Silu
        raise KeyError(kind)

    def phase1(self, l, xin):
        A, P, NH, T_, TP = self.A, self.P, self.NH, self.T, self.TP
        self.begin()
        xT = A.tile(KC * TP, BF16)
        xTv = xT.ap.rearrange("p (k t) -> p k t", k=KC)
        xst = [A.tile(D // 2, F32) for _ in range(4)]
        wb = [A.tile(KC * 512, BF16) for _ in range(2)]
        ev32 = [A.tile(512, F32) for _ in range(3)]
        ev16 = [A.tile(512, BF16) for _ in range(3)]
        ps_tr = self.psum_tiles([6, 7], 512)
        acc = self.psum_tiles([0, 1, 2, 3, 4, 5], 512)
        nev = [0]
        nw = [0]
        nstg = [0]

        def load_w(si):
            w = wb[nw[0] % 2]
            nw[0] += 1
            src = self.I["w1"][l, si]
            q = KC * 512 // 8
            for i in range(8):
                k = nstg[0]; nstg[0] += 1
                self.ld_cast(w, w.ap[:, i * q:(i + 1) * q], src[:, i * q:(i + 1) * q], xst[k % 4],
                             "dve" if k % 2 == 0 else "act")
            return w

        NSFd = self.dbg.get('nsf', self.NSF); NSTd = self.dbg.get('nst', self.NST)
        seq = []
        for p in range(T_ // TP):
            seq += [(p, 'f', si) for si in range(NSFd)] + [(p, 't', sj) for sj in range(NSTd)]
        wl = {}

        def ensure(k):
            if k < len(seq) and k not in wl:
                _, kind_, s_ = seq[k]
                wl[k] = load_w(s_ if kind_ == 'f' else self.NSF + s_)
        kpos = 0
        for p in range(T_ // TP):
            t0 = p * TP
            for tt in range(TP // 128):
                xh = [xst[(2 * tt) % 4], xst[(2 * tt + 1) % 4]]
                for hh in range(2):
                    self.ld(xh[hh], xh[hh].ap, xin[t0 + tt * 128:t0 + (tt + 1) * 128, hh * (D // 2):(hh + 1) * (D // 2)])
                for g in range(KC // 4):
                    pt = ps_tr[g % 2]
                    for j in range(4):
                        kc = g * 4 + j
                        xs = xh[kc // (KC // 2)]
                        kk = kc % (KC // 2)
                        self.tr(pt, pt.ap[:, j * 128:(j + 1) * 128], xs, xs.ap[:, kk * 128:(kk + 1) * 128], self.ident)
                    dst = xTv[:, g * 4:(g + 1) * 4, tt * 128:(tt + 1) * 128]
                    src = pt.ap.rearrange("p (k t) -> p k t", k=4)
                    self.cp("dve" if g % 2 == 0 else "act", xT, dst, pt, src)
            na = 0
            for si in range(NSFd):
                ensure(kpos); ensure(kpos + 1)
                w = wl.pop(kpos); kpos += 1
                wv = w.ap.rearrange("p (k j) -> p k j", k=KC)
                for sub in range(4):
                    kind, idx = self.fmb[si * 4 + sub]
                    dst, ddt, fn = self.fm_dest(kind, idx)
                    accs = []
                    for t2 in range(TP // 512):
                        accs.append(acc[na % 6]); na += 1
                    for kc in range(KC):
                        for t2 in range(TP // 512):
                            self.mm(accs[t2], accs[t2].ap, w, wv[:, kc, sub * 128:(sub + 1) * 128],
                                    xT, xTv[:, kc, t2 * 512:(t2 + 1) * 512], start=(kc == 0), stop=(kc == KC - 1))
                    for t2 in range(TP // 512):
                        i = nev[0]; nev[0] += 1
                        ev = (ev32 if ddt == F32 else ev16)[i % 3]
                        if fn is not None:
                            self.act(ev, ev.ap, accs[t2], accs[t2].ap, fn)
                        else:
                            self.cp("act" if i % 2 == 0 else "dve", ev, ev.ap, accs[t2], accs[t2].ap)
                        self.st(dst[:, t0 + t2 * 512:t0 + (t2 + 1) * 512], ev, ev.ap, q="act")
            for sj in range(NSTd):
                ensure(kpos); ensure(kpos + 1)
                w = wl.pop(kpos); kpos += 1
                wv = w.ap.rearrange("p (k j) -> p k j", k=KC)
                half = self.NST // 2
                dst = self.S["Av"] if sj < half else self.S["Dv"]
                c0 = (sj % half) * 512
                for tt in range(TP // 128):
                    a = acc[na % 6]; na += 1
                    for kc in range(KC):
                        self.mm(a, a.ap, xT, xTv[:, kc, tt * 128:(tt + 1) * 128], w, wv[:, kc, :],
                                start=(kc == 0), stop=(kc == KC - 1))
                    i = nev[0]; nev[0] += 1
                    ev = ev16[i % 3]
                    self.cp("act" if i % 2 == 0 else "dve", ev, ev.ap, a, a.ap)
                    self.st(dst[t0 + tt * 128:t0 + (tt + 1) * 128, c0:c0 + 512], ev, ev.ap, q="act")
        self.end()

    def post_setup(self, small=False):
        A = self.A
        n8, n12, n4 = (2, 4, 1) if small else (8, 12, 4)
        self.ptr = self.psum_tiles([6], 64, BF16)
        self.osb = [A.tile(128, F32) for _ in range(n8)]
        self.onb = [A.tile(128, BF16) for _ in range(n8)]
        self.cols = [A.tile(8, F32) for _ in range(n12)]
        self.sqs = [A.tile(128, F32) for _ in range(n4)]
        self.cnt = {"s": 0, "e": 0, "t": 0, "o": 0, "c": 0, "p": 0, "g": 0, "x": 0, "q": 0}

    def attn_setup(self):
        A = self.A
        self.post_setup()
        self.sbank = self.psum_tiles([0, 1, 7], 512)
        self.oacc = [Sub(self.bk[b], self.banks[b][:, 0:132]) for b in (2, 3, 4, 5)]
        self.E = [A.tile(512, BF16) for _ in range(self.dbg.get('depth', 4) + 2)]
        self.tb = [A.tile(128, F32) for _ in range(3)]

    def nxt(self, lst, key):
        i = self.cnt[key]; self.cnt[key] += 1
        return lst[i % len(lst)]

    def attn_q(self, oacc, Vaug, kbs, smm_fn, scale, bias_fn, done_cb=None):
        Vt, Vv = Vaug
        ngrp = (len(kbs) + 3) // 4
        for gi in range(ngrp):
            grp = kbs[gi * 4:gi * 4 + 4]
            sb = self.nxt(self.sbank, "s")
            for i, kb in enumerate(grp):
                mms = smm_fn(kb)
                for j, (lt, lap, rt, rap) in enumerate(mms):
                    self.mm(sb, sb.ap[:, i * 128:(i + 1) * 128], lt, lap, rt, rap, start=(j == 0), stop=(j == len(mms) - 1))
            pq_ = self.__dict__.setdefault("_pq", [])
            while len(pq_) >= self.dbg.get('depth', 4):
                pq_.pop(0)()
            self.advance_fin()
            E = self.nxt(self.E, "e")
            i = 0
            while i < len(grp):
                b = bias_fn(grp[i])
                if b is None:
                    j = i
                    while j < len(grp) and bias_fn(grp[j]) is None:
                        j += 1
                    self.act(E, E.ap[:, i * 128:j * 128], sb, sb.ap[:, i * 128:j * 128], AF.Exp, scale=scale)
                    i = j
                elif b[0] == "col":
                    self.act(E, E.ap[:, i * 128:(i + 1) * 128], sb, sb.ap[:, i * 128:(i + 1) * 128], AF.Exp,
                             bias=b[2], scale=scale, bt=b[1])
                    i += 1
                else:
                    tb = self.nxt(self.tb, "t")
                    self.stt(tb, tb.ap, sb, sb.ap[:, i * 128:(i + 1) * 128], scale, b[1], b[2], ALU.mult, ALU.add)
                    self.act(E, E.ap[:, i * 128:(i + 1) * 128], tb, tb.ap, AF.Exp)
                    i += 1
            last = (gi == ngrp - 1)

            def pv(grp=grp, E=E, first=(gi == 0), last=last):
                for i, kb in enumerate(grp):
                    self.mm(oacc, oacc.ap[:, 0:129], E, E.ap[:, i * 128:(i + 1) * 128], Vt, Vv[:, kb, 0:129],
                            start=(first and i == 0), stop=(last and i == len(grp) - 1))
                if last and done_cb is not None:
                    done_cb()
            self._pq.append(pv)

    def attn_group(self, oaccs, Vaug, qg, smm_fn, scale, diag_bias, done_cb, diag_sep):
        Vt, Vv = Vaug
        nkb = 4 * qg + 4
        for kb in range(nkb):
            d = kb - 4 * qg if kb >= 4 * qg else None
            qlo = 0 if d is None else d
            sb = self.nxt(self.sbank, "s")
            if diag_sep and d is not None:
                c0w = (d + 1) * 128
            else:
                c0w = qlo * 128
            if c0w < 512:
                mms = smm_fn(kb, c0w, 512, False)
                for j, (lt, lap, rt, rap) in enumerate(mms):
                    self.mm(sb, sb.ap[:, c0w:512], lt, lap, rt, rap, start=(j == 0), stop=(j == len(mms) - 1))
            if diag_sep and d is not None:
                mms = smm_fn(kb, d * 128, (d + 1) * 128, True)
                for j, (lt, lap, rt, rap) in enumerate(mms):
                    self.mm(sb, sb.ap[:, d * 128:(d + 1) * 128], lt, lap, rt, rap, start=(j == 0), stop=(j == len(mms) - 1))
            pq_ = self.__dict__.setdefault("_pq", [])
            while len(pq_) >= self.dbg.get('depth', 4):
                pq_.pop(0)()
            self.advance_fin()
            E = self.nxt(self.E, "e")
            e0 = (d + 1) * 128 if d is not None else 0
            if e0 < 512:
                self.act(E, E.ap[:, e0:512], sb, sb.ap[:, e0:512], AF.Exp, scale=scale)
            if d is not None:
                tb = self.nxt(self.tb, "t")
                ds_ = slice(d * 128, (d + 1) * 128)
                self.stt(tb, tb.ap, sb, sb.ap[:, ds_], scale, diag_bias[0], diag_bias[1], ALU.mult, ALU.add)
                self.act(E, E.ap[:, ds_], tb, tb.ap, AF.Exp)

            def pv(kb=kb, E=E, qlo=qlo):
                for qi in range(qlo, 4):
                    last = (kb == 4 * qg + qi)
                    self.mm(oaccs[qi], oaccs[qi].ap[:, 0:129], E, E.ap[:, qi * 128:(qi + 1) * 128], Vt, Vv[:, kb, 0:129],
                            start=(kb == 0), stop=last)
                    if last:
                        done_cb(qi)
            self._pq.append(pv)

    def attn_flush(self):
        pq_ = self.__dict__.setdefault("_pq", [])
        while pq_:
            pq_.pop(0)()
        self.advance_fin(drain=True)

    def finish_o_gen(self, o_t, normw, gate_t, oTst, qb, pre_scale=None):
        cols = self.nxt(self.cols, "c")
        onb = self.nxt(self.onb, "o")
        if pre_scale is not None:
            sq = self.nxt(self.sqs, "q")
            self.tt(sq, sq.ap, o_t, o_t.ap, o_t, o_t.ap, ALU.mult)
            self.rsum(cols, cols.ap[:, 0:1], sq, sq.ap)
            yield
            self.act(cols, cols.ap[:, 1:2], cols, cols.ap[:, 0:1], AF.Ln, bias=self.eps6.ap, scale=1.0 / 128, bt=self.eps6)
            self.act(cols, cols.ap[:, 2:3], cols, cols.ap[:, 1:2], AF.Exp, scale=-0.5)
            yield
            self.ts(onb, onb.ap, o_t, o_t.ap, cols.ap[:, 2:3], ALU.mult, pre_scale, ALU.mult, st=cols)
        else:
            self.cp("dve", onb, onb.ap, o_t, o_t.ap)
        pt = self.nxt(self.ptr, "p")
        self.tr(pt, pt.ap, onb, onb.ap, self.identb)
        yield
        dst = oTst.ap[:, qb * 128:(qb + 1) * 128]
        gap = gate_t.ap[:, qb * 128:(qb + 1) * 128]
        if normw is not None:
            self.stt(oTst, dst, pt, pt.ap, normw.ap, gate_t, gap, ALU.mult, ALU.mult, st=normw)
        else:
            self.tt(oTst, dst, pt, pt.ap, gate_t, gap, ALU.mult)

    def finish_o(self, *a, **k):
        self.__dict__.setdefault("_fin", []).append(self.finish_o_gen(*a, **k))

    def finish_o_now(self, *a, **k):
        for _ in self.finish_o_gen(*a, **k):
            pass

    def advance_fin(self, drain=False):
        fin = self.__dict__.setdefault("_fin", [])
        while True:
            for g_ in list(fin):
                try:
                    next(g_)
                except StopIteration:
                    fin.remove(g_)
            if not drain or not fin:
                break

    def load_vaug(self, Vt, src_tok_major, h):
        Vv = Vt.ap.rearrange("p (b c) -> p b c", c=132)
        src = src_tok_major[:, h * 128:(h + 1) * 128].rearrange("(b p) c -> p b c", p=128)
        self.ld(Vt, Vv[:, :, 0:128], src)
        self.mset(Vt, Vv[:, :, 128:129], 1.0, eng="pool")
        return Vv

    def phaseA(self, l):
        A, NH, T_, NB = self.A, self.NH, self.T, self.NB
        self.begin()
        self.attn_setup()
        QA = [[A.tile(T_, BF16) for _ in range(2)] for _ in range(2)]
        KA = [[A.tile(T_, BF16) for _ in range(2)] for _ in range(2)]
        augs = A.tile(2 * T_, F32)
        Vt = [A.tile(NB * 132, BF16) for _ in range(2)]
        G = [A.tile(T_, BF16) for _ in range(2)]
        oTst = [A.tile(T_, BF16) for _ in range(2)]
        diag = [A.tile(128, F32) for _ in range(2)]
        o0s = [A.tile(128, F32) for _ in range(4)]
        dnorm = A.tile(1, F32)
        lamt = A.tile(256, F32)
        lcol = A.tile(8, F32)
        tmp64 = A.tile(64, F32)
        self.ld(dnorm, dnorm.ap, self.I["dnorm"][l])
        self.ld(lamt, lamt.ap, self.I["dlam"][l].partition_broadcast(128))
        self.tt(tmp64, tmp64.ap, lamt, lamt.ap[:, 0:64], lamt, lamt.ap[:, 64:128], ALU.mult)
        self.rsum(lcol, lcol.ap[:, 0:1], tmp64, tmp64.ap)
        self.tt(tmp64, tmp64.ap, lamt, lamt.ap[:, 128:192], lamt, lamt.ap[:, 192:256], ALU.mult)
        self.rsum(lcol, lcol.ap[:, 1:2], tmp64, tmp64.ap)
        self.act(lcol, lcol.ap[:, 2:4], lcol, lcol.ap[:, 0:2], AF.Exp)
        self.stt(lcol, lcol.ap[:, 4:5], lcol, lcol.ap[:, 3:4], -LAM_INIT[l], lcol, lcol.ap[:, 2:3], ALU.add, ALU.subtract)
        neglam = lcol.ap[:, 4:5]
        Vvs = {}

        def load_head(h):
            b = h % 2
            self.ld(augs, augs.ap[64:67, 0:T_], self.I["c_augK"][h])
            self.ld(augs, augs.ap[64:67, T_:2 * T_], self.I["c_augQ"][h])
            for m_ in range(2):
                self.ld(KA[b][m_], KA[b][m_].ap[0:64, :], self.S["AkT"][h][64 * m_:64 * m_ + 64, :])
                self.ld(QA[b][m_], QA[b][m_].ap[0:64, :], self.S["AqT"][h][64 * m_:64 * m_ + 64, :])
                self.cp("pool", KA[b][m_], KA[b][m_].ap[64:67, :], augs, augs.ap[64:67, 0:T_])
                self.cp("pool", QA[b][m_], QA[b][m_].ap[64:67, :], augs, augs.ap[64:67, T_:2 * T_])
            self.ld(G[b], G[b].ap, self.S["gate"][0 * NH + h])
            self.ld(diag[b], diag[b].ap, self.I["c_diagA"][h])
            Vvs[h] = self.load_vaug(Vt[b], self.S["Av"], h)
        load_head(0)
        for h in range(NH):
            b = h % 2
            if h + 1 < NH:
                load_head(h + 1)
            Vv = Vvs[h]
            for qg in range(NB // 4):
                for m in range(2):
                    ka, qa = KA[b][m], QA[b][m]

                    def smm_fn(kb, c0, c1, diag, ka=ka, qa=qa, qg=qg):
                        nr = 64 if diag else 67
                        return [(ka, ka.ap[0:nr, kb * 128:(kb + 1) * 128], qa, qa.ap[0:nr, qg * 512 + c0:qg * 512 + c1])]

                    def done(qi, m=m, qg=qg, b=b):
                        acc = self.oacc[qi]
                        cols = self.nxt(self.cols, "c")
                        self.recip(cols, cols.ap[:, 0:1], acc, acc.ap[:, 128:129])
                        if m == 0:
                            self.ts(o0s[qi], o0s[qi].ap, acc, acc.ap[:, 0:128], cols.ap[:, 0:1], ALU.mult, st=cols)
                        else:
                            o = self.nxt(self.osb, "o")
                            self.tt(cols, cols.ap[:, 2:3], cols, cols.ap[:, 0:1], lcol, neglam, ALU.mult)
                            self.stt(o, o.ap, acc, acc.ap[:, 0:128], cols.ap[:, 2:3], o0s[qi], o0s[qi].ap, ALU.mult, ALU.add, st=cols)
                            self.finish_o(o, dnorm, G[b], oTst[b], qg * 4 + qi, pre_scale=1.0 - LAM_INIT[l])
                    self.attn_group(self.oacc, (Vt[b], Vv), qg, smm_fn, 0.125, (diag[b], diag[b].ap), done, True)
            self.attn_flush()
            self.st(self.S["oT"][0 * NH + h], oTst[b], oTst[b].ap)
        self.end()

    def phaseD(self, l):
        A, NH, T_, NB = self.A, self.NH, self.T, self.NB
        self.begin()
        self.attn_setup()
        QT = [A.tile(T_, BF16) for _ in range(2)]
        KT = [A.tile(T_, BF16) for _ in range(2)]
        Vt = [A.tile(NB * 132, BF16) for _ in range(2)]
        G = [A.tile(T_, BF16) for _ in range(2)]
        oTst = [A.tile(T_, BF16) for _ in range(2)]
        bias = [A.tile(640, F32) for _ in range(2)]
        mask = A.tile(640, F32)
        self.ld(mask, mask.ap, self.I["c_maskD"])
        scale = 128 ** -0.5
        Vvs = {}

        def load_head(h):
            b = h % 2
            self.ld(QT[b], QT[b].ap, self.S["DqT"][h])
            self.ld(KT[b], KT[b].ap, self.S["DkT"][h])
            self.ld(G[b], G[b].ap, self.S["gate"][3 * NH + h])
            self.ld(bias[b], bias[b].ap, self.I["biasD"][l, h])
            self.tt(bias[b], bias[b].ap, bias[b], bias[b].ap, mask, mask.ap, ALU.add, eng="pool")
            Vvs[h] = self.load_vaug(Vt[b], self.S["Dv"], h)
        load_head(0)
        for h in range(NH):
            b = h % 2
            if h + 1 < NH:
                load_head(h + 1)
            Vv = Vvs[h]
            for qb in range(NB):
                acc = self.nxt(self.oacc, "o")
                kbs = [kb for kb in range(qb - 4, qb + 1) if kb >= 0]
                smm_fn = lambda kb, qb=qb: [(KT[b], KT[b].ap[:, kb * 128:(kb + 1) * 128],
                                             QT[b], QT[b].ap[:, qb * 128:(qb + 1) * 128])]
                bias_fn = lambda kb, qb=qb: ("tile", bias[b], bias[b].ap[:, (qb - kb) * 128:(qb - kb + 1) * 128])
                def done(acc=acc, qb=qb, b=b):
                    cols = self.nxt(self.cols, "c")
                    o = self.nxt(self.osb, "o")
                    self.recip(cols, cols.ap[:, 0:1], acc, acc.ap[:, 128:129])
                    self.ts(o, o.ap, acc, acc.ap[:, 0:128], cols.ap[:, 0:1], ALU.mult, st=cols)
                    self.finish_o(o, None, G[b], oTst[b], qb)
                self.attn_q(acc, (Vt[b], Vv), kbs, smm_fn, scale, bias_fn, done_cb=done)
            self.attn_flush()
            self.st(self.S["oT"][3 * NH + h], oTst[b], oTst[b].ap)
        self.end()

    def phaseC(self, l):
        A, NH, T_, NB = self.A, self.NH, self.T, self.NB
        self.begin()
        self.attn_setup()
        NT5 = T_ // 512
        cqn = A.tile(6 * T_, BF16); cqnv = cqn.ap.rearrange("p (k t) -> p k t", k=6)
        ckvn = A.tile(2 * T_, BF16); ckvnv = ckvn.ap.rearrange("p (k t) -> p k t", k=2)
        kro = A.tile(T_, BF16, parts=64)
        rope = [A.tile(T_, F32, parts=64) for _ in range(2)]
        qn = A.tile(6, F32); kvn = A.tile(2, F32)
        maskC = A.tile(128, F32)
        wuq = [A.tile(6 * 256, BF16) for _ in range(2)]
        wk = [A.tile(2 * 128, BF16) for _ in range(2)]
        wv = A.tile(2 * NH * 128, BF16)
        QnT = A.tile(T_, BF16); QrT = A.tile(T_, BF16, parts=64); KnT = A.tile(T_, BF16)
        Vt = A.tile(NB * 132, BF16)
        G = [A.tile(T_, BF16) for _ in range(2)]
        oTst = [A.tile(T_, BF16) for _ in range(2)]
        stg = [A.tile(512, F32) for _ in range(2)]
        sqt = [A.tile(512, F32) for _ in range(2)]
        rbc = A.tile(512, F32)
        t64 = [A.tile(512, F32, parts=64) for _ in range(2)]
        big = self.psum_tiles([5], 512)[0]
        oacc4 = [Sub(self.bk[b_], self.banks[b_][:, 0:132]) for b_ in (2, 3, 4, 5)]
        self.ld(qn, qn.ap, self.I["qnorm"][l]); self.ld(kvn, kvn.ap, self.I["kvnorm"][l])
        self.ld(maskC, maskC.ap, self.I["c_maskC"])
        self.ld(rope[0], rope[0].ap, self.I["c_rope"][0]); self.ld(rope[1], rope[1].ap, self.I["c_rope"][1])
        wstg = A.tile(2 * NH * 128, F32)
        self.ld_cast(wv, wv.ap, self.I["wukvv"][l], wstg, "pool")
        wvv = wv.ap.rearrange("p (k c) -> p k c", k=2)
        for (src, nk, nrm, dstv, dim) in ((self.S["Ccq"], 6, qn, cqnv, 768), (self.S["Cckv"], 2, kvn, ckvnv, 256)):
            for t5 in range(NT5):
                ts_ = slice(t5 * 512, (t5 + 1) * 512)
                for k in range(nk):
                    s = stg[k % 2]
                    self.ld(s, s.ap, src[k][:, ts_])
                    q = sqt[k % 2]
                    self.tt(q, q.ap, s, s.ap, s, s.ap, ALU.mult)
                    self.mm(big, big.ap, self.ones, self.ones.ap, q, q.ap, start=(k == 0), stop=(k == nk - 1))
                self.act(rbc, rbc.ap, big, big.ap, AF.Sqrt, bias=self.eps6.ap, scale=1.0 / dim, bt=self.eps6)
                self.recip(rbc, rbc.ap, rbc, rbc.ap)
                for k in range(nk):
                    s = stg[k % 2]
                    self.ld(s, s.ap, src[k][:, ts_])
                    self.stt((cqn if nk == 6 else ckvn), dstv[:, k, ts_], s, s.ap, nrm.ap[:, k:k + 1], rbc, rbc.ap,
                             ALU.mult, ALU.mult, st=nrm)
        for t5 in range(NT5):
            ts_ = slice(t5 * 512, (t5 + 1) * 512)
            a, b2 = t64[0], t64[1]
            self.ld(a, a.ap, self.S["Ckr"][0][0:64, ts_])
            self.ld(b2, b2.ap, self.S["Ckr"][1][0:64, ts_])
            self.tt(a, a.ap, a, a.ap, rope[0], rope[0].ap[:, ts_], ALU.mult)
            self.tt(b2, b2.ap, b2, b2.ap, rope[1], rope[1].ap[:, ts_], ALU.mult)
            self.tt(kro, kro.ap[:, ts_], a, a.ap, b2, b2.ap, ALU.add)
        scale = 192 ** -0.5
        pq = big
        def load_head(h):
            b = h % 2
            self.ld_cast(wuq[b], wuq[b].ap, self.I["wuq"][l, h], wstg, "pool")
            self.ld_cast(wk[b], wk[b].ap, self.I["wukvk"][l, h], wstg, "pool")
            self.ld(G[b], G[b].ap, self.S["gate"][2 * NH + h])
        load_head(0)
        for h in range(NH):
            b = h % 2
            if h + 1 < NH:
                load_head(h + 1)
            wq = wuq[b].ap.rearrange("p (k c) -> p k c", k=6)
            wkk = wk[b].ap.rearrange("p (k c) -> p k c", k=2)
            for t5 in range(NT5):
                ts_ = slice(t5 * 512, (t5 + 1) * 512)
                for k in range(6):
                    self.mm(pq, pq.ap, wuq[b], wq[:, k, 0:128], cqn, cqnv[:, k, ts_], start=(k == 0), stop=(k == 5))
                self.cp("act", QnT, QnT.ap[:, ts_], pq, pq.ap)
                a, b2 = t64[0], t64[1]
                for k in range(6):
                    self.mm(pq, pq.ap[0:64, :], wuq[b], wq[:, k, 128:192], cqn, cqnv[:, k, ts_], start=(k == 0), stop=(k == 5))
                self.tt(a, a.ap, pq, pq.ap[0:64, :], rope[0], rope[0].ap[:, ts_], ALU.mult)
                for k in range(6):
                    self.mm(pq, pq.ap[0:64, :], wuq[b], wq[:, k, 192:256], cqn, cqnv[:, k, ts_], start=(k == 0), stop=(k == 5))
                self.tt(b2, b2.ap, pq, pq.ap[0:64, :], rope[1], rope[1].ap[:, ts_], ALU.mult)
                self.tt(QrT, QrT.ap[:, ts_], a, a.ap, b2, b2.ap, ALU.add)
                for k in range(2):
                    self.mm(pq, pq.ap, wk[b], wkk[:, k, :], ckvn, ckvnv[:, k, ts_], start=(k == 0), stop=(k == 1))
                self.cp("act", KnT, KnT.ap[:, ts_], pq, pq.ap)
            Vv = Vt.ap.rearrange("p (b c) -> p b c", c=132)
            self.mset(Vt, Vv[:, :, 128:129], 1.0, eng="pool")
            for tb_ in range(NB):
                pv = self.nxt(self.sbank, "s")
                for k in range(2):
                    self.mm(pv, pv.ap[:, 0:128], ckvn, ckvnv[:, k, tb_ * 128:(tb_ + 1) * 128], wv, wvv[:, k, h * 128:(h + 1) * 128],
                            start=(k == 0), stop=(k == 1))
                self.cp("dve", Vt, Vv[:, tb_, 0:128], pv, pv.ap[:, 0:128])
            for qg in range(NB // 4):
                def smm_fn(kb, c0, c1, diag, qg=qg):
                    ks = slice(kb * 128, (kb + 1) * 128); qs = slice(qg * 512 + c0, qg * 512 + c1)
                    return [(KnT, KnT.ap[:, ks], QnT, QnT.ap[:, qs]), (kro, kro.ap[:, ks], QrT, QrT.ap[:, qs])]

                def done(qi, qg=qg, b=b):
                    acc = oacc4[qi]
                    cols = self.nxt(self.cols, "c")
                    o = self.nxt(self.osb, "o")
                    self.recip(cols, cols.ap[:, 0:1], acc, acc.ap[:, 128:129])
                    self.ts(o, o.ap, acc, acc.ap[:, 0:128], cols.ap[:, 0:1], ALU.mult, st=cols)
                    self.finish_o(o, None, G[b], oTst[b], qg * 4 + qi)
                self.attn_group(oacc4, (Vt, Vv), qg, smm_fn, scale, (maskC, maskC.ap), done, False)
            self.attn_flush()
            self.st(self.S["oT"][2 * NH + h], oTst[b], oTst[b].ap)
        self.end()


    def phaseB(self, l):
        A, NH, T_, NB = self.A, self.NH, self.T, self.NB
        self.begin()
        self.post_setup(small=True)
        mmr = lambda *a, **k: self.mm(*a, r=True, **k)
        NT5 = T_ // 512
        big = lambda: A.tile(T_, F32)
        t_q, t_k, t_v, tmp1, tmp2 = big(), big(), big(), big(), big()
        gbc, bbc, egbc, kbT, kbgT, qgT = big(), big(), big(), big(), big(), big()
        kdT = bbc
        Gt = A.tile(T_, F32, parts=NH); nGt = A.tile(T_, F32, parts=NH); beta = A.tile(T_, F32, parts=NH)
        sel = A.tile(NH * 128, F32, parts=NH)
        tri = A.tile(3 * 128, F32)
        gnorm = A.tile(1, F32)
        gcols = A.tile(4, F32, parts=NH)
        cw = A.tile(12, F32)
        eglast = A.tile(NB, F32)
        XL = [A.tile(256, F32) for _ in range(NB)]
        qkL = [A.tile(128, F32) for _ in range(NB)]
        wL = [A.tile(128, F32) for _ in range(NB)]
        kdL = [A.tile(128, F32) for _ in range(NB)]
        NBATCH = 4
        Mbb = [[[A.tile(128, F32) for _ in range(NBATCH)] for _ in range(2)] for _ in range(2)]
        MTbb = [[[A.tile(128, F32) for _ in range(NBATCH)] for _ in range(2)] for _ in range(2)]
        DTbb = [[A.tile(128, F32) for _ in range(NBATCH)] for _ in range(2)]
        Sst = A.tile(128, F32)
        vnew = [A.tile(128, F32) for _ in range(2)]
        Gg = A.tile(T_, BF16); oTst = A.tile(T_, BF16)
        gtl = [self.psum_tiles([0, 1], 128), self.psum_tiles([4, 5], 128)]
        xtl = [self.psum_tiles([2, 3], 256), self.psum_tiles([6, 7], 256)]
        bcp = self.psum_tiles([4, 5], 512)
        sc4 = [Sub(self.bk[b_], self.banks[b_][:, 0:128]) for b_ in (0, 1, 2, 3)]
        self.ld(sel, sel.ap, self.I["c_sel"])
        self.ld(tri, tri.ap.rearrange("p (k c) -> p k c", k=3), self.I["c_tri"].rearrange("k p c -> p k c"))
        U_incl, U_strict = tri.ap[:, 0:128], tri.ap[:, 128:256]
        self.ld(gnorm, gnorm.ap, self.I["gnorm"][l])
        self.ld(gcols, gcols.ap[:, 0:1], self.I["alog"][l]); self.ld(gcols, gcols.ap[:, 1:2], self.I["dtb"][l])
        a_t = T(tmp1.ap[0:NH, :]); c_t = T(tmp2.ap[0:NH, :])
        self.ld(a_t, a_t.ap, self.S["Bab"][0][0:NH, :])
        self.ld(beta, beta.ap, self.S["Bab"][1][0:NH, :])
        self.act(beta, beta.ap, beta, beta.ap, AF.Sigmoid)
        self.act(a_t, a_t.ap, a_t, a_t.ap, AF.Exp, bias=gcols.ap[:, 1:2], bt=gcols)
        self.act(a_t, a_t.ap, a_t, a_t.ap, AF.Ln, bias=1.0)
        self.act(gcols, gcols.ap[:, 2:3], gcols, gcols.ap[:, 0:1], AF.Exp)
        self.ts(gcols, gcols.ap[:, 3:4], gcols, gcols.ap[:, 2:3], -1.0, ALU.mult)
        self.ts(a_t, a_t.ap, a_t, a_t.ap, gcols.ap[:, 3:4], ALU.mult, st=gcols)
        src, dst = a_t, c_t
        sh = 1
        while sh < 128:
            sv = src.ap.rearrange("p (b c) -> p b c", c=128); dv = dst.ap.rearrange("p (b c) -> p b c", c=128)
            self.cp("pool", dst, dv[:, :, 0:sh], src, sv[:, :, 0:sh])
            self.tt(dst, dv[:, :, sh:128], src, sv[:, :, sh:128], src, sv[:, :, 0:128 - sh], ALU.add)
            src, dst = dst, src
            sh *= 2
        self.cp("dve", Gt, Gt.ap, src, src.ap)
        self.ts(nGt, nGt.ap, src, src.ap, -1.0, ALU.mult)
        self.P.barrier()
        qgTs = [qgT, A.tile(T_, F32)]
        eglasts = [eglast, A.tile(NB, F32)]

        def prologue_gen(h):
            selh = sel.ap[:, h * 128:(h + 1) * 128]
            qgT = qgTs[h % 2]; eglast = eglasts[h % 2]
            for ty, dstt in enumerate((t_q, t_k, t_v)):
                self.ld(tmp1, tmp1.ap, self.S["Bpre"][ty * NH + h])
                self.ld(cw, cw.ap[:, ty * 4:ty * 4 + 4], self.I["convw"][l, ty * NH + h])
                w = lambda j: cw.ap[:, ty * 4 + j:ty * 4 + j + 1]
                self.ts(tmp2, tmp2.ap, tmp1, tmp1.ap, w(3), ALU.mult, st=cw)
                yield
                for j in range(3):
                    s_ = 3 - j
                    self.stt(tmp2, tmp2.ap[:, s_:T_], tmp1, tmp1.ap[:, 0:T_ - s_], w(j), tmp2, tmp2.ap[:, s_:T_],
                             ALU.mult, ALU.add, st=cw)
                    yield
                self.act(dstt, dstt.ap, tmp2, tmp2.ap, AF.Silu)
                yield
            for tq, extra in ((t_q, 128 ** -0.5), (t_k, 1.0)):
                self.tt(tmp1, tmp1.ap, tq, tq.ap, tq, tq.ap, ALU.mult)
                yield
                for t5 in range(NT5):
                    ts_ = slice(t5 * 512, (t5 + 1) * 512)
                    ps = bcp[t5 % 2]
                    mmr(ps, ps.ap, self.ones, self.ones.ap, tmp1, tmp1.ap[:, ts_])
                    self.act(tmp2, tmp2.ap[:, ts_], ps, ps.ap, AF.Sqrt, bias=self.eps6.ap, bt=self.eps6)
                    yield
                self.recip(tmp2, tmp2.ap, tmp2, tmp2.ap)
                yield
                self.stt(tq, tq.ap, tq, tq.ap, extra, tmp2, tmp2.ap, ALU.mult, ALU.mult)
                yield
            for (srcg, dstb) in ((Gt, gbc), (beta, bbc)):
                for t5 in range(NT5):
                    ts_ = slice(t5 * 512, (t5 + 1) * 512)
                    ps = bcp[t5 % 2]
                    mmr(ps, ps.ap, sel, selh, srcg, srcg.ap[:, ts_])
                    self.cp("act" if t5 % 2 else "dve", dstb, dstb.ap[:, ts_], ps, ps.ap)
                    yield
            self.act(egbc, egbc.ap, gbc, gbc.ap, AF.Exp)
            self.tt(kbT, kbT.ap, t_k, t_k.ap, bbc, bbc.ap, ALU.mult)
            yield
            self.tt(kbgT, kbgT.ap, kbT, kbT.ap, egbc, egbc.ap, ALU.mult, eng="pool")
            self.tt(qgT, qgT.ap, t_q, t_q.ap, egbc, egbc.ap, ALU.mult)
            yield
            self.tt(t_v, t_v.ap, t_v, t_v.ap, bbc, bbc.ap, ALU.mult, eng="pool")
            gv = gbc.ap.rearrange("p (b c) -> p b c", c=128)
            self.act(eglast, eglast.ap, gbc, gv[:, :, 127], AF.Exp)
            yield
            for j in range(NB):
                bs = slice(j * 128, (j + 1) * 128)
                self.act(tmp1, tmp1.ap[:, bs], gbc, gbc.ap[:, bs], AF.Exp, bias=gbc.ap[:, j * 128 + 127:j * 128 + 128], scale=-1.0, bt=gbc)
                if j % 4 == 3:
                    yield
            self.tt(kdT, kdT.ap, t_k, t_k.ap, tmp1, tmp1.ap, ALU.mult)
            yield

        for _ in prologue_gen(0):
            pass
        for h in range(NH):
            selh = sel.ap[:, h * 128:(h + 1) * 128]
            qgT = qgTs[h % 2]; eglast = eglasts[h % 2]
            self.ld(Gg, Gg.ap, self.S["gate"][1 * NH + h])
            def batch_gen(js, gt, xt_, Mb, MTb, DTb):
                cnt = {"g": 0, "x": 0}

                def ng():
                    cnt["g"] += 1
                    return gt[(cnt["g"] - 1) % len(gt)]

                def nx():
                    cnt["x"] += 1
                    return xt_[(cnt["x"] - 1) % len(xt_)]
                cur = 0
                pss = {}
                for i, j in enumerate(js):
                    bs = slice(j * 128, (j + 1) * 128)
                    p = ng(); pss[i] = p
                    mmr(p, p.ap, sel, selh, Gt, Gt.ap[:, bs], start=True, stop=False)
                    mmr(p, p.ap, nGt, nGt.ap[:, bs], sel, selh, start=False, stop=True)
                yield
                for i, j in enumerate(js):
                    self.ts(DTb[i], DTb[i].ap, pss[i], pss[i].ap, 0.0, ALU.min)
                    self.act(DTb[i], DTb[i].ap, DTb[i], DTb[i].ap, AF.Exp)
                yield
                for i, j in enumerate(js):
                    bs = slice(j * 128, (j + 1) * 128)
                    p = ng(); pss[i] = p
                    mmr(p, p.ap, t_k, t_k.ap[:, bs], kbT, kbT.ap[:, bs])
                yield
                for i, j in enumerate(js):
                    LT = MTb[cur][i]
                    self.tt(LT, LT.ap, pss[i], pss[i].ap, DTb[i], DTb[i].ap, ALU.mult)
                    self.tt(LT, LT.ap, LT, LT.ap, tri, U_strict, ALU.mult, eng="pool")
                yield
                for i, j in enumerate(js):
                    bs = slice(j * 128, (j + 1) * 128)
                    p = ng(); pss[i] = p
                    mmr(p, p.ap, t_k, t_k.ap[:, bs], t_q, t_q.ap[:, bs])
                yield
                for i, j in enumerate(js):
                    self.tt(qkL[j], qkL[j].ap, pss[i], pss[i].ap, DTb[i], DTb[i].ap, ALU.mult)
                    self.tt(qkL[j], qkL[j].ap, qkL[j], qkL[j].ap, tri, U_incl, ALU.mult, eng="pool")
                yield
                for i, j in enumerate(js):
                    p = ng(); pss[i] = p
                    self.tr(p, p.ap, MTb[cur][i], MTb[cur][i].ap, self.ident)
                yield
                for i, j in enumerate(js):
                    self.cp("act", Mb[cur][i], Mb[cur][i].ap, pss[i], pss[i].ap)
                yield
                xps = {}
                for i, j in enumerate(js):
                    bs = slice(j * 128, (j + 1) * 128)
                    p = nx(); xps[i] = p
                    self.tr(p, p.ap[:, 0:128], t_v, t_v.ap[:, bs], self.ident)
                    self.tr(p, p.ap[:, 128:256], kbgT, kbgT.ap[:, bs], self.ident)
                yield
                for i, j in enumerate(js):
                    self.cp("dve", XL[j], XL[j].ap, xps[i], xps[i].ap)
                yield
                for i, j in enumerate(js):
                    bs = slice(j * 128, (j + 1) * 128)
                    p = ng(); pss[i] = p
                    self.tr(p, p.ap, kdT, kdT.ap[:, bs], self.ident)
                yield
                for i, j in enumerate(js):
                    self.cp("act", kdL[j], kdL[j].ap, pss[i], pss[i].ap)
                yield
                for lev in range(7):
                    if lev > 0:
                        nxt_ = 1 - cur
                        p2 = {}
                        for i, j in enumerate(js):
                            p = ng(); pss[i] = p
                            mmr(p, p.ap, Mb[cur][i], Mb[cur][i].ap, MTb[cur][i], MTb[cur][i].ap)
                            if lev < 6:
                                q_ = ng(); p2[i] = q_
                                mmr(q_, q_.ap, MTb[cur][i], MTb[cur][i].ap, Mb[cur][i], Mb[cur][i].ap)
                        yield
                        for i, j in enumerate(js):
                            self.cp("act", MTb[nxt_][i], MTb[nxt_][i].ap, pss[i], pss[i].ap)
                            if lev < 6:
                                self.cp("dve", Mb[nxt_][i], Mb[nxt_][i].ap, p2[i], p2[i].ap)
                        yield
                        cur = nxt_
                    for i, j in enumerate(js):
                        p = nx(); xps[i] = p
                        mmr(p, p.ap, MTb[cur][i], MTb[cur][i].ap, XL[j], XL[j].ap)
                    yield
                    for i, j in enumerate(js):
                        self.tt(XL[j], XL[j].ap, XL[j], XL[j].ap, xps[i], xps[i].ap,
                                ALU.subtract if lev == 0 else ALU.add)
                    yield
                for i, j in enumerate(js):
                    p = ng(); pss[i] = p
                    self.tr(p, p.ap, XL[j], XL[j].ap[:, 128:256], self.ident)
                yield
                for i, j in enumerate(js):
                    self.cp("act", wL[j], wL[j].ap, pss[i], pss[i].ap)
                yield

            for j0 in range(0, NB, 2 * NBATCH):
                gens = []
                for bi in range(2):
                    js = list(range(j0 + bi * NBATCH, min(NB, j0 + (bi + 1) * NBATCH)))
                    if js:
                        gens.append(batch_gen(js, gtl[bi], xtl[bi], Mbb[bi], MTbb[bi], DTbb[bi]))
                while gens:
                    for g_ in list(gens):
                        try:
                            next(g_)
                        except StopIteration:
                            gens.remove(g_)
            npg = prologue_gen(h + 1) if h + 1 < NH else iter(())
            self.mset(Sst, Sst.ap, 0.0)
            for j in range(NB):
                for _ in range(3):
                    next(npg, None)
                bs = slice(j * 128, (j + 1) * 128)
                vn = vnew[j % 2]
                p1 = sc4[(3 * j) % 4]; po = sc4[(3 * j + 1) % 4]; pS = sc4[(3 * j + 2) % 4]
                mmr(p1, p1.ap, wL[j], wL[j].ap, Sst, Sst.ap)
                self.tt(vn, vn.ap, XL[j], XL[j].ap[:, 0:128], p1, p1.ap, ALU.subtract)
                mmr(po, po.ap, qgT, qgT.ap[:, bs], Sst, Sst.ap, start=True, stop=False)
                mmr(po, po.ap, qkL[j], qkL[j].ap, vn, vn.ap, start=False, stop=True)
                mmr(pS, pS.ap, kdL[j], kdL[j].ap, vn, vn.ap)
                self.stt(Sst, Sst.ap, Sst, Sst.ap, eglast.ap[:, j:j + 1], pS, pS.ap, ALU.mult, ALU.add, st=eglast)
                o = self.nxt(self.osb, "o")
                self.cp("act", o, o.ap, po, po.ap)
                self.finish_o_now(o, gnorm, Gg, oTst, j, pre_scale=1.0)
            for _ in npg:
                pass
            self.st(self.S["oT"][1 * NH + h], oTst, oTst.ap)
        self.end()

    def phase3(self, l):
        A, NH, T_, TP = self.A, self.NH, self.T, self.TP
        self.begin()
        NK = 4 * NH
        oT = A.tile(NK * TP, BF16); oTv = oT.ap.rearrange("p (k t) -> p k t", k=NK)
        wb = [A.tile(NK * 512, BF16) for _ in range(2)]
        wst = [A.tile(NK * 512 // 8, F32) for _ in range(4)]
        ev = [A.tile(512, F32) for _ in range(4)]
        acc = self.psum_tiles(list(range(8)), 512)
        ne = 0
        nstg = 0
        seq3 = [(p, db) for p in range(T_ // TP) for db in range(8)]
        wl3 = {}

        def ensure3(k):
            nonlocal nstg
            if k < len(seq3) and k not in wl3:
                _, db_ = seq3[k]
                w_ = wb[k % 2]
                src = self.I["wout"][l, db_]
                q = NK * 512 // 8
                for i in range(8):
                    self.ld_cast(w_, w_.ap[:, i * q:(i + 1) * q], src[:, i * q:(i + 1) * q], wst[nstg % 4],
                                 "dve" if nstg % 2 == 0 else "act")
                    nstg += 1
                wl3[k] = w_
        k3 = 0
        for p in range(T_ // TP):
            t0 = p * TP
            for k in range(NK):
                self.ld(oT, oTv[:, k, :], self.S["oT"][k][:, t0:t0 + TP])
            for db in range(8):
                ensure3(k3); ensure3(k3 + 1)
                w = wl3.pop(k3); k3 += 1
                wv = w.ap.rearrange("p (k j) -> p k j", k=NK)
                ntt = TP // 128
                for k in range(NK):
                    for tt in range(ntt):
                        self.mm(acc[tt], acc[tt].ap, oT, oTv[:, k, tt * 128:(tt + 1) * 128], w, wv[:, k, :],
                                start=(k == 0), stop=(k == NK - 1))
                for tt in range(ntt):
                    e = ev[ne % 4]; ne += 1
                    self.cp("act" if ne % 2 == 0 else "dve", e, e.ap, acc[tt], acc[tt].ap)
                    self.st(self.S["y"][t0 + tt * 128:t0 + (tt + 1) * 128, db * 512:(db + 1) * 512], e, e.ap, q="act")
        self.end()

    def phaseN(self, l, xin, xout, final):
        A, T_ = self.A, self.T
        self.begin()
        gb = A.tile(D, F32); bb = A.tile(D, F32)
        self.ld(gb, gb.ap, self.I["lng"][l].partition_broadcast(128))
        self.ld(bb, bb.ap, self.I["lnb"][l].partition_broadcast(128))
        xt = [A.tile(D, F32) for _ in range(2)]
        yt = [A.tile(D, F32) for _ in range(2)]
        zt = [A.tile(D, F32) for _ in range(2)]
        cols = [A.tile(8, F32) for _ in range(2)]
        def load_tile(tt):
            b = tt % 2
            rows = slice(tt * 128, (tt + 1) * 128)
            self.ld(xt[b], xt[b].ap, xin[rows, :])
            self.ld(yt[b], yt[b].ap, self.S["y"][rows, :])
        load_tile(0)
        for tt in range(T_ // 128):
            b = tt % 2
            rows = slice(tt * 128, (tt + 1) * 128)
            x_, y_, z_, c_ = xt[b], yt[b], zt[b], cols[b]
            if tt + 1 < T_ // 128:
                load_tile(tt + 1)
            self.stt(z_, z_.ap, x_, x_.ap, ALPHA, y_, y_.ap, ALU.mult, ALU.add)
            self.rsum(c_, c_.ap[:, 0:1], z_, z_.ap)
            self.act(y_, y_.ap, z_, z_.ap, AF.Square)
            self.rsum(c_, c_.ap[:, 2:3], y_, y_.ap)
            self.ts(c_, c_.ap[:, 1:2], c_, c_.ap[:, 0:1], 1.0 / D, ALU.mult)
            self.tt(c_, c_.ap[:, 3:4], c_, c_.ap[:, 1:2], c_, c_.ap[:, 1:2], ALU.mult)
            self.stt(c_, c_.ap[:, 5:6], c_, c_.ap[:, 2:3], 1.0 / D, c_, c_.ap[:, 3:4], ALU.mult, ALU.subtract)
            self.act(c_, c_.ap[:, 6:7], c_, c_.ap[:, 5:6], AF.Ln, bias=self.eps5.ap, bt=self.eps5)
            self.act(c_, c_.ap[:, 4:5], c_, c_.ap[:, 6:7], AF.Exp, scale=-0.5)
            self.stt(z_, z_.ap, z_, z_.ap, c_.ap[:, 1:2], gb, gb.ap, ALU.subtract, ALU.mult, st=c_)
            self.stt(z_, z_.ap, z_, z_.ap, c_.ap[:, 4:5], bb, bb.ap, ALU.mult, ALU.add, st=c_)
            self.st(xout[rows, :], z_, z_.ap, final=final)
        self.end()


def _cols(NH, hs):
    ar = np.arange(128)
    ar64 = np.arange(64)
    blocks = []
    for kind, idx in fm_blocks(NH):
        if kind == "Aq": c = O_AQ + hs[idx] * 128 + ar
        elif kind == "Ak": c = O_AK + hs[idx] * 128 + ar
        elif kind == "Bq": c = O_BQ + hs[idx] * 128 + ar
        elif kind == "Bk": c = O_BK + hs[idx] * 128 + ar
        elif kind == "Bv": c = O_BV + hs[idx] * 128 + ar
        elif kind == "Ccq": c = O_CQ + idx * 128 + ar
        elif kind == "Cckv": c = O_CKV + idx * 128 + ar
        elif kind == "Ckr":
            r = ar64 if idx == 0 else (ar64 + 32) % 64
            c = O_CKR + np.concatenate([r, r])
        elif kind == "Bab":
            base = O_BA if idx == 0 else O_BB
            c = np.array([base + h for h in hs] + [base + hs[0]] * (128 - NH))
        elif kind == "Dq": c = O_DQ + hs[idx] * 128 + ar
        elif kind == "Dk": c = O_DK + hs[idx] * 128 + ar
        elif kind == "G":
            m, hh = divmod(idx, NH)
            c = O_G + m * 1024 + hs[hh] * 128 + ar
        blocks.append(c)
    av = np.concatenate([O_AV + h * 128 + ar for h in hs])
    dv = np.concatenate([O_DV + h * 128 + ar for h in hs])
    return np.concatenate(blocks + [av, dv])


def prep_shared(inp, NH, hs, T_, L):
    f32 = np.float32
    out = {}
    inp = {k: (v if k == "x" else v[:L]) for k, v in inp.items()}
    cols = _cols(NH, hs)
    ns1 = len(cols) // 512
    w1 = np.empty((L, ns1, 128, KC * 512), f32)
    for l in range(L):
        wsel = inp["w_in"][l][:, cols]
        w1[l] = wsel.reshape(KC, 128, ns1, 512).transpose(2, 1, 0, 3).reshape(ns1, 128, KC * 512)
    out["w1"] = w1
    NK = 4 * NH
    mixrows = np.concatenate([m * 1024 + h * 128 + np.arange(128) for m in range(4) for h in hs])
    wout = np.empty((L, 8, 128, NK * 512), f32)
    for l in range(L):
        ws = inp["w_out"][l][mixrows]
        wout[l] = ws.reshape(NK, 128, 8, 512).transpose(2, 1, 0, 3).reshape(8, 128, NK * 512)
    out["wout"] = wout
    wuq = np.empty((L, NH, 128, 6, 256), f32)
    wk = np.empty((L, NH, 128, 2, 128), f32)
    wv = np.empty((L, 128, 2, NH, 128), f32)
    sw = (np.arange(64) + 32) % 64
    for l in range(L):
        uq = inp["mla_w_uq"][l].reshape(6, 128, 8, 192)
        ukv = inp["mla_w_ukv"][l].reshape(2, 128, 8, 256)
        for i, h in enumerate(hs):
            wuq[l, i, :, :, 0:128] = uq[:, :, h, 0:128].transpose(1, 0, 2)
            wuq[l, i, :, :, 128:192] = uq[:, :, h, 128:192].transpose(1, 0, 2)
            wuq[l, i, :, :, 192:256] = uq[:, :, h, 128:192][:, :, sw].transpose(1, 0, 2)
            wk[l, i] = ukv[:, :, h, 0:128].transpose(1, 0, 2)
            wv[l, :, :, i, :] = ukv[:, :, h, 128:256].transpose(1, 0, 2)
    out["wuq"] = wuq.reshape(L, NH, 128, 6 * 256)
    out["wukvk"] = wk.reshape(L, NH, 128, 256)
    out["wukvv"] = wv.reshape(L, 128, 2 * NH * 128)
    out["qnorm"] = np.ascontiguousarray(inp["mla_q_norm"].reshape(L, 6, 128).transpose(0, 2, 1))
    out["kvnorm"] = np.ascontiguousarray(inp["mla_kv_norm"].reshape(L, 2, 128).transpose(0, 2, 1))
    out["dnorm"] = np.ascontiguousarray(inp["diff_norm"].reshape(L, 128, 1))
    out["gnorm"] = np.ascontiguousarray(inp["gdn_norm"].reshape(L, 128, 1))
    out["dlam"] = np.ascontiguousarray(inp["diff_lambda"].reshape(L, 1, 256))
    out["alog"] = np.ascontiguousarray(inp["gdn_a_log"][:, hs].reshape(L, NH, 1))
    out["dtb"] = np.ascontiguousarray(inp["gdn_dt_bias"][:, hs].reshape(L, NH, 1))
    cw = np.empty((L, 3 * NH, 128, 4), f32)
    for l in range(L):
        g = inp["gdn_conv"][l].reshape(4, 3, 8, 128)
        for ty in range(3):
            for i, h in enumerate(hs):
                cw[l, ty * NH + i] = g[:, ty, h, :].T
    out["convw"] = cw
    ki = np.arange(128)[:, None]
    qi = np.arange(128)[None, :]
    bd = np.empty((L, NH, 128, 5, 128), f32)
    md = np.empty((128, 5, 128), f32)
    for dl in range(5):
        idx = np.clip(128 * dl + qi - ki, -128, 128) + 128
        for l in range(L):
            for i, h in enumerate(hs):
                bd[l, i, :, dl, :] = inp["rel_bias"][l, h][idx]
        cd = 2 * dl + (qi >= 64).astype(int) - (ki >= 64).astype(int)
        md[:, dl, :] = np.where((cd >= 0) & (cd <= 8), 0.0, NEG)
    out["biasD"] = bd.reshape(L, NH, 128, 640)
    out["c_maskD"] = md.reshape(128, 640)
    out["lng"] = np.ascontiguousarray(inp["ln_gain"].reshape(L, 1, D))
    out["lnb"] = np.ascontiguousarray(inp["ln_bias"].reshape(L, 1, D))
    slopes = 2.0 ** (-8.0 * np.arange(1, 9) / 8.0)
    augK = np.empty((NH, 3, T_), f32)
    augQ = np.empty((NH, 3, T_), f32)
    tpos = np.arange(T_)
    dg = np.empty((NH, 128, 128), f32)
    vis = (ki // 64) <= (qi // 64)
    for i, h in enumerate(hs):
        sl = slopes[h]
        augK[i] = np.stack([8.0 * sl * (tpos % 128), 1024.0 * sl * (tpos // 128), np.ones(T_)])
        augQ[i] = np.stack([np.ones(T_), np.ones(T_), -1024.0 * sl * (tpos // 128)])
        dg[i] = np.where(vis, -sl * np.abs(qi - ki) + sl * qi, NEG)
    out["c_augK"] = augK
    out["c_augQ"] = augQ
    out["c_diagA"] = dg
    out["c_maskC"] = np.where(vis, 0.0, NEG).astype(f32)
    inv = 10000.0 ** (-np.arange(32, dtype=np.float64) / 32.0)
    ang = (np.arange(T_, dtype=np.float64)[None, :] * inv.astype(np.float32).astype(np.float64)[:, None])
    ang = np.concatenate([ang, ang], axis=0)
    sgn = np.concatenate([-np.ones(32), np.ones(32)])[:, None]
    out["c_rope"] = np.stack([np.cos(ang), sgn * np.sin(ang)]).astype(f32)
    sel = np.zeros((NH, NH, 128), f32)
    for i in range(NH):
        sel[i, i, :] = 1.0
    out["c_sel"] = sel.reshape(NH, NH * 128)
    s_ = np.arange(128)[:, None]; c_ = np.arange(128)[None, :]
    out["c_tri"] = np.stack([(c_ >= s_), (c_ > s_), (c_ < s_)]).astype(f32)
    out["c_ident"] = np.eye(128, dtype=f32)
    return out
from concourse.bass_utils import run_bass_kernel_spmd

NH_CFG = 8
T_CFG = 2048
L_CFG = 2


def build_nc(NH, T_, L, dbg=None, arena_kib=200):
    nc = bass.Bass("TRN2", target_bir_lowering=False)
    b = Builder(nc, NH, T_, L, dbg=dbg, arena_kib=arena_kib)
    b.build()
    return nc, b


def kernel(**inputs):
    inp = {k: np.asarray(v) for k, v in inputs.items()}
    NH, T_, L = NH_CFG, T_CFG, L_CFG
    hs = list(range(8))
    shared = prep_shared(inp, NH, hs, T_, L)
    nc, _ = build_nc(NH, T_, L)
    in_maps = []
    for c in range(8):
        m = dict(shared)
        m["x"] = np.ascontiguousarray(inp["x"][c % 4])
        in_maps.append(m)
    res = run_bass_kernel_spmd(nc, in_maps, core_ids=list(range(8)))
    out = np.stack([np.asarray(res.results[b]["out"]) for b in range(4)]).astype(np.float32)
    return out
```

```python
import numpy as np
import concourse.bass as bass
import concourse.mybir as mybir

F32 = mybir.dt.float32
BF16 = mybir.dt.bfloat16
U8 = mybir.dt.uint8
AF = mybir.ActivationFunctionType
ALU = mybir.AluOpType
AX = mybir.AxisListType

STREAMS = ("pe", "act", "dve", "pool", "sp")
NDMA_SEM = 8


class T:
    __slots__ = ("ap", "last_w", "readers", "rc")

    def __init__(self, ap):
        self.ap = ap
        self.last_w = None
        self.readers = []
        self.rc = {}

    def __getitem__(self, k):
        return self.ap[k]


class Sub:
    psum = True

    def __init__(self, par, ap):
        self.par = par
        self.ap = ap

    last_w = property(lambda s: s.par.last_w, lambda s, v: setattr(s.par, "last_w", v))
    readers = property(lambda s: s.par.readers, lambda s, v: setattr(s.par, "readers", v))
    rc = property(lambda s: s.par.rc, lambda s, v: setattr(s.par, "rc", v))


class Op:
    __slots__ = ("stream", "fn", "idx", "dma", "deps", "signaled", "rank", "dma_j")

    def __init__(self, stream, fn, idx, dma):
        self.stream = stream
        self.fn = fn
        self.idx = idx
        self.dma = dma
        self.deps = []
        self.signaled = False
        self.rank = 0
        self.dma_j = -1


class Prog:
    def __init__(self, nc, same_engine_sync=True):
        self.nc = nc
        self.ops = {s: [] for s in STREAMS}
        self.ndma = {s: 0 for s in STREAMS}
        self.dma_ops = {s: [] for s in STREAMS}
        self.fence = {s: None for s in STREAMS}
        self.same_engine_sync = same_engine_sync
        self.final_dmas = []

    def op(self, stream, fn, reads=(), writes=(), dma=False, final=False):
        o = Op(stream, fn, len(self.ops[stream]), dma)
        deps = []
        pr = [t for t in reads if getattr(t, "psum", False)]
        if pr:
            reads = [t for t in reads if not getattr(t, "psum", False)]
            writes = list(writes) + [t for t in pr if t not in writes]
        for t in reads:
            if t.last_w is not None:
                deps.append(t.last_w)
        for t in writes:
            if t.last_w is not None:
                deps.append(t.last_w)
            deps.extend(t.readers)
            deps.extend(t.rc.values())
        if self.fence[stream] is not None:
            deps.extend(self.fence[stream])
            self.fence[stream] = None
        seen = set()
        for d in deps:
            if d is o or id(d) in seen:
                continue
            seen.add(id(d))
            if (not d.dma) and d.stream == stream:
                if stream == "pe" or not self.same_engine_sync:
                    continue
            o.deps.append(d)
        for t in reads:
            if dma:
                t.readers.append(o)
            else:
                t.rc[stream] = o
        for t in writes:
            t.last_w = o
            t.readers = []
            t.rc = {}
        if dma:
            o.dma_j = self.ndma[stream]
            self.ndma[stream] += 1
            self.dma_ops[stream].append(o)
            if final:
                self.final_dmas.append(o)
        self.ops[stream].append(o)
        return o

    def barrier(self):
        deps = []
        for s in STREAMS:
            if self.ops[s]:
                for o in reversed(self.ops[s]):
                    if not o.dma:
                        deps.append(o)
                        break
            deps.extend(self.dma_ops[s][-NDMA_SEM:])
        for s in STREAMS:
            self.fence[s] = list(deps) + (self.fence[s] or [])

    def emit(self):
        nc = self.nc
        if self.final_dmas:
            fo = Op("sp", None, len(self.ops["sp"]), False)
            fo.deps = list(self.final_dmas)
            self.ops["sp"].append(fo)
        for s in STREAMS:
            for o in self.ops[s]:
                for d in o.deps:
                    if not d.dma:
                        d.signaled = True
        for s in STREAMS:
            r = 0
            for o in self.ops[s]:
                if o.signaled and not o.dma:
                    r += 1
                    o.rank = r
        from contextlib import ExitStack
        with ExitStack() as es:
            esem = {s: es.enter_context(nc.semaphore("S_" + s)) for s in STREAMS}
            dsem = {s: [es.enter_context(nc.semaphore("D_%s_%d" % (s, i))) for i in range(NDMA_SEM)]
                    for s in STREAMS if self.ndma[s] > 0}
            block = es.enter_context(nc.Block())

            def run_stream(s, eng):
                waited = {}
                for o in self.ops[s]:
                    need = {}
                    for d in o.deps:
                        if d.dma:
                            sem = dsem[d.stream][d.dma_j % NDMA_SEM]
                            val = 16 * (d.dma_j // NDMA_SEM + 1)
                        else:
                            sem = esem[d.stream]
                            val = d.rank
                        k = id(sem)
                        if k not in need or need[k][1] < val:
                            need[k] = (sem, val)
                    if o.dma and o.dma_j >= NDMA_SEM:
                        sem = dsem[s][o.dma_j % NDMA_SEM]
                        val = 16 * (o.dma_j // NDMA_SEM)
                        k = id(sem)
                        if k not in need or need[k][1] < val:
                            need[k] = (sem, val)
                    for k, (sem, val) in need.items():
                        if waited.get(k, 0) >= val:
                            continue
                        waited[k] = val
                        eng.wait_ge(sem, val)
                    if o.fn is None:
                        continue
                    ins = o.fn(eng)
                    if o.dma:
                        ins.then_inc(dsem[s][o.dma_j % NDMA_SEM], 16)
                    elif o.signaled:
                        ins.then_inc(esem[s], 1)

            @block.tensor
            def _(e):
                run_stream("pe", e)

            @block.scalar
            def _(e):
                run_stream("act", e)

            @block.vector
            def _(e):
                run_stream("dve", e)

            @block.gpsimd
            def _(e):
                run_stream("pool", e)

            @block.sync
            def _(e):
                run_stream("sp", e)

    def stats(self):
        return {s: len(self.ops[s]) for s in STREAMS}


class Arena:
    def __init__(self, base_ap, nbytes):
        self.base = base_ap
        self.nbytes = nbytes
        self.off = 0
        self.marks = []

    def alloc(self, cols, dtype, parts=128):
        esz = {F32: 4, BF16: 2, U8: 1}[dtype]
        n = cols * esz
        self.off = (self.off + 31) // 32 * 32
        assert self.off + n <= self.nbytes, ("SBUF arena overflow", self.off, n)
        v = self.base[0:parts, self.off:self.off + n]
        if dtype != U8:
            v = v.bitcast(dtype)
        self.off += n
        return v

    def tile(self, cols, dtype, parts=128):
        return T(self.alloc(cols, dtype, parts))

    def mark(self):
        self.marks.append(self.off)

    def release(self):
        self.off = self.marks.pop()
import math
from contextlib import ExitStack

D = 4096
KC = D // 128
ALPHA = 4.0 ** 0.25
LAM_INIT = [0.8 - 0.6 * math.exp(-0.3 * l) for l in range(2)]
NEG = -30000.0

O_AQ, O_AK, O_AV = 0, 1024, 2048
O_BQ, O_BK, O_BV, O_BA, O_BB = 3072, 4096, 5120, 6144, 6152
O_CQ, O_CKV, O_CKR = 6160, 6928, 7184
O_DQ, O_DK, O_DV, O_G = 7248, 8272, 9296, 10320


def fm_blocks(NH):
    bl = []
    for h in range(NH): bl.append(("Aq", h))
    for h in range(NH): bl.append(("Ak", h))
    for h in range(NH): bl.append(("Bq", h))
    for h in range(NH): bl.append(("Bk", h))
    for h in range(NH): bl.append(("Bv", h))
    for i in range(6): bl.append(("Ccq", i))
    for i in range(2): bl.append(("Cckv", i))
    bl.append(("Ckr", 0)); bl.append(("Ckr", 1))
    bl.append(("Bab", 0)); bl.append(("Bab", 1))
    for h in range(NH): bl.append(("Dq", h))
    for h in range(NH): bl.append(("Dk", h))
    for m in range(4):
        for h in range(NH): bl.append(("G", m * NH + h))
    assert len(bl) % 4 == 0
    return bl


class Builder:
    def __init__(self, nc, NH, T_, L, dbg=None, arena_kib=176):
        self.nc, self.NH, self.T, self.L = nc, NH, T_, L
        self.dbg = dbg or {}
        self.arena_kib = arena_kib
        self.NB = T_ // 128
        self.TP = min(1024, T_)
        self.fmb = fm_blocks(NH)
        self.NSF = len(self.fmb) // 4
        self.NST = 2 * (NH * 128 // 512)
        self.NS1 = self.NSF + self.NST

    def declare(self):
        nc, NH, T_, L = self.nc, self.NH, self.T, self.L
        di = lambda n, s, dt=F32: nc.dram_tensor(n, s, dt, kind="ExternalInput").ap()
        ds = lambda n, s, dt: nc.dram_tensor(n, s, dt, kind="Internal").ap()
        I = {}
        I["x"] = di("x", [T_, D])
        I["w1"] = di("w1", [L, self.NS1, 128, KC * 512])
        I["wout"] = di("wout", [L, 8, 128, 4 * NH * 512])
        I["wuq"] = di("wuq", [L, NH, 128, 6 * 256])
        I["wukvk"] = di("wukvk", [L, NH, 128, 2 * 128])
        I["wukvv"] = di("wukvv", [L, 128, 2 * NH * 128])
        I["qnorm"] = di("qnorm", [L, 128, 6])
        I["kvnorm"] = di("kvnorm", [L, 128, 2])
        I["dnorm"] = di("dnorm", [L, 128, 1])
        I["gnorm"] = di("gnorm", [L, 128, 1])
        I["dlam"] = di("dlam", [L, 1, 256])
        I["alog"] = di("alog", [L, NH, 1])
        I["dtb"] = di("dtb", [L, NH, 1])
        I["convw"] = di("convw", [L, 3 * NH, 128, 4])
        I["biasD"] = di("biasD", [L, NH, 128, 5 * 128])
        I["lng"] = di("lng", [L, 1, D])
        I["lnb"] = di("lnb", [L, 1, D])
        I["c_maskD"] = di("c_maskD", [128, 5 * 128])
        I["c_augK"] = di("c_augK", [NH, 3, T_])
        I["c_augQ"] = di("c_augQ", [NH, 3, T_])
        I["c_diagA"] = di("c_diagA", [NH, 128, 128])
        I["c_maskC"] = di("c_maskC", [128, 128])
        I["c_rope"] = di("c_rope", [2, 64, T_])
        I["c_sel"] = di("c_sel", [NH, NH * 128])
        I["c_tri"] = di("c_tri", [3, 128, 128])
        I["c_ident"] = di("c_ident", [128, 128])
        self.I = I
        S = {}
        for n in ("AqT", "AkT", "DqT", "DkT"):
            S[n] = ds("S_" + n, [NH, 128, T_], BF16)
        S["Av"] = ds("S_Av", [T_, NH * 128], BF16)
        S["Dv"] = ds("S_Dv", [T_, NH * 128], BF16)
        S["Bpre"] = ds("S_Bpre", [3 * NH, 128, T_], F32)
        S["Bab"] = ds("S_Bab", [2, 128, T_], F32)
        S["Ccq"] = ds("S_Ccq", [6, 128, T_], F32)
        S["Cckv"] = ds("S_Cckv", [2, 128, T_], F32)
        S["Ckr"] = ds("S_Ckr", [2, 128, T_], F32)
        S["gate"] = ds("S_gate", [4 * NH, 128, T_], BF16)
        if "oT" in self.dbg:
            S["oT"] = nc.dram_tensor("S_oT", [4 * NH, 128, T_], BF16, kind="ExternalOutput").ap()
        else:
            S["oT"] = ds("S_oT", [4 * NH, 128, T_], BF16)
        S["y"] = ds("S_y", [T_, D], F32)
        S["x1"] = ds("S_x1", [T_, D], F32)
        self.S = S
        self.out = nc.dram_tensor("out", [T_, D], F32, kind="ExternalOutput").ap()
        for n in self.dbg:
            if n in ("oT",):
                continue
            if n in S:
                pass

    def mm(self, ot, oap, lt, lap, rt, rap, start=True, stop=True, r=False):
        if r and self.dbg.get("f32r", False):
            lap = lap.bitcast(mybir.dt.float32r); rap = rap.bitcast(mybir.dt.float32r)
        self.P.op("pe", lambda e: e.matmul(oap, lhsT=lap, rhs=rap, start=start, stop=stop),
                  reads=[lt, rt], writes=[ot])

    def tr(self, ot, oap, it, iap, ident):
        self.P.op("pe", lambda e: e.transpose(oap, iap, ident.ap), reads=[it, ident], writes=[ot])

    def act(self, ot, oap, it, iap, func, bias=None, scale=1.0, bt=None, eng="act"):
        rd = [it] + ([bt] if bt is not None else [])
        if bias is None:
            self.P.op(eng, lambda e: e.activation(out=oap, in_=iap, func=func, scale=scale), reads=rd, writes=[ot])
        else:
            self.P.op(eng, lambda e: e.activation(out=oap, in_=iap, func=func, bias=bias, scale=scale), reads=rd, writes=[ot])

    def cp(self, eng, ot, oap, it, iap):
        if eng == "act":
            self.P.op("act", lambda e: e.copy(out=oap, in_=iap), reads=[it], writes=[ot])
        else:
            self.P.op(eng, lambda e: e.tensor_copy(out=oap, in_=iap), reads=[it], writes=[ot])

    def tt(self, ot, oap, at, aap, bt, bap, op, eng="dve"):
        self.P.op(eng, lambda e: e.tensor_tensor(out=oap, in0=aap, in1=bap, op=op), reads=[at, bt], writes=[ot])

    def ts(self, ot, oap, it, iap, s1, op0, s2=None, op1=None, st=None, eng="dve"):
        rd = [it] + ([st] if st is not None else [])
        if op1 is None:
            self.P.op(eng, lambda e: e.tensor_scalar(out=oap, in0=iap, scalar1=s1, scalar2=None, op0=op0), reads=rd, writes=[ot])
        else:
            self.P.op(eng, lambda e: e.tensor_scalar(out=oap, in0=iap, scalar1=s1, scalar2=s2, op0=op0, op1=op1), reads=rd, writes=[ot])

    def stt(self, ot, oap, at, aap, sc, bt, bap, op0, op1, st=None, eng="dve"):
        rd = [at, bt] + ([st] if st is not None else [])
        self.P.op(eng, lambda e: e.scalar_tensor_tensor(out=oap, in0=aap, scalar=sc, in1=bap, op0=op0, op1=op1),
                  reads=rd, writes=[ot])

    def rsum(self, ot, oap, it, iap, eng="dve"):
        self.P.op(eng, lambda e: e.reduce_sum(out=oap, in_=iap, axis=AX.X), reads=[it], writes=[ot])

    def recip(self, ot, oap, it, iap):
        self.P.op("dve", lambda e: e.reciprocal(out=oap, in_=iap), reads=[it], writes=[ot])

    def mset(self, ot, oap, val, eng="dve"):
        self.P.op(eng, lambda e: e.memset(oap, val), writes=[ot])

    def ld(self, ot, oap, src, q="sp", reads=()):
        return self.P.op(q, lambda e: e.dma_start(out=oap, in_=src), reads=list(reads), writes=[ot], dma=True)

    def st(self, dst, it, iap, q="sp", final=False):
        return self.P.op(q, lambda e: e.dma_start(out=dst, in_=iap), reads=[it], dma=True, final=final)

    def ld_cast(self, dst_t, dst_ap, src, stg, eng):
        n = dst_ap.shape[-1] if len(dst_ap.shape) == 2 else None
        sap = stg.ap[0:dst_ap.shape[0], 0:n]
        self.ld(stg, sap, src)
        self.cp(eng, dst_t, dst_ap, stg, sap)

    def rstd_col(self, out_t, in_t, scale, eps_t):
        self.act(out_t, out_t.ap, in_t, in_t.ap, AF.Sqrt, bias=eps_t.ap[0:out_t.ap.shape[0], :], scale=scale, bt=eps_t)
        self.recip(out_t, out_t.ap, out_t, out_t.ap)

    def build(self):
        nc = self.nc
        self.declare()
        with ExitStack() as es:
            nbytes = self.arena_kib * 1024
            sb = es.enter_context(nc.sbuf_tensor("arena", [128, nbytes], U8))
            self.banks = [es.enter_context(nc.psum_tensor("bank%d" % i, [128, 512], F32)) for i in range(8)]
            self.A = Arena(sb, nbytes)
            self.P = Prog(nc)
            A = self.A
            self.ident = A.tile(128, F32)
            self.identb = A.tile(128, BF16)
            self.ones = A.tile(128, F32)
            self.eps6 = A.tile(1, F32)
            self.eps5 = A.tile(1, F32)
            self.ld(self.ident, self.ident.ap, self.I["c_ident"])
            self.cp("dve", self.identb, self.identb.ap, self.ident, self.ident.ap)
            self.mset(self.ones, self.ones.ap, 1.0)
            self.mset(self.eps6, self.eps6.ap, 1e-6)
            self.mset(self.eps5, self.eps5.ap, 1e-5)
            phases = self.dbg.get("phases", "1ABCD3N")
            for l in range(self.L):
                xin = self.I["x"] if l == 0 else self.S["x1"]
                xout = self.out if l == self.L - 1 else self.S["x1"]
                if "1" in phases: self.phase1(l, xin)
                if "A" in phases: self.phaseA(l)
                if "C" in phases: self.phaseC(l)
                if "D" in phases: self.phaseD(l)
                if "B" in phases: self.phaseB(l)
                if "3" in phases: self.phase3(l)
                if "N" in phases: self.phaseN(l, xin, xout, final=(l == self.L - 1))
            self.P.emit()
        return nc

    def begin(self):
        self.P.barrier()
        self.A.mark()
        self.bk = [T(self.banks[i][:, :]) for i in range(8)]

    def end(self):
        if self.dbg.get("mem"):
            print("arena peak(approx cur) KiB:", self.A.off / 1024.0)
        self.A.release()
        self.P.barrier()

    def psum_tiles(self, bank_ids, cols, dtype=F32):
        out = []
        for b in bank_ids:
            for c0 in range(0, 512 - cols + 1, cols):
                ap = self.banks[b][:, c0:c0 + cols]
                if dtype == BF16:
                    ap = ap.bitcast(BF16)
                out.append(Sub(self.bk[b], ap))
        return out

    def fm_dest(self, kind, idx):
        S = self.S
        NH = self.NH
        if kind in ("Aq", "Ak", "Dq", "Dk"):
            return S[kind + "T"][idx], BF16, None
        if kind in ("Bq", "Bk", "Bv"):
            return S["Bpre"][{"Bq": 0, "Bk": 1, "Bv": 2}[kind] * NH + idx], F32, None
        if kind == "Ccq": return S["Ccq"][idx], F32, None
        if kind == "Cckv": return S["Cckv"][idx], F32, None
        if kind == "Ckr": return S["Ckr"][idx], F32, None
        if kind == "Bab": return S["Bab"][idx], F32, None
        if kind == "G": return S["gate"][idx], BF16, AF.Silu
        raise KeyError(kind)

    def phase1(self, l, xin):
        A, P, NH, T_, TP = self.A, self.P, self.NH, self.T, self.TP
        self.begin()
        xT = A.tile(KC * TP, BF16)
        xTv = xT.ap.rearrange("p (k t) -> p k t", k=KC)
        xst = [A.tile(D // 2, F32) for _ in range(4)]
        wb = [A.tile(KC * 512, BF16) for _ in range(2)]
        ev32 = [A.tile(512, F32) for _ in range(3)]
        ev16 = [A.tile(512, BF16) for _ in range(3)]
        ps_tr = self.psum_tiles([6, 7], 512)
        acc = self.psum_tiles([0, 1, 2, 3, 4, 5], 512)
        nev = [0]
        nw = [0]
        nstg = [0]

        def load_w(si):
            w = wb[nw[0] % 2]
            nw[0] += 1
            src = self.I["w1"][l, si]
            q = KC * 512 // 8
            for i in range(8):
                k = nstg[0]; nstg[0] += 1
                self.ld_cast(w, w.ap[:, i * q:(i + 1) * q], src[:, i * q:(i + 1) * q], xst[k % 4],
                             "dve" if k % 2 == 0 else "act")
            return w

        NSFd = self.dbg.get('nsf', self.NSF); NSTd = self.dbg.get('nst', self.NST)
        seq = []
        for p in range(T_ // TP):
            seq += [(p, 'f', si) for si in range(NSFd)] + [(p, 't', sj) for sj in range(NSTd)]
        wl = {}

        def ensure(k):
            if k < len(seq) and k not in wl:
                _, kind_, s_ = seq[k]
                wl[k] = load_w(s_ if kind_ == 'f' else self.NSF + s_)
        kpos = 0
        for p in range(T_ // TP):
            t0 = p * TP
            for tt in range(TP // 128):
                xh = [xst[(2 * tt) % 4], xst[(2 * tt + 1) % 4]]
                for hh in range(2):
                    self.ld(xh[hh], xh[hh].ap, xin[t0 + tt * 128:t0 + (tt + 1) * 128, hh * (D // 2):(hh + 1) * (D // 2)])
                for g in range(KC // 4):
                    pt = ps_tr[g % 2]
                    for j in range(4):
                        kc = g * 4 + j
                        xs = xh[kc // (KC // 2)]
                        kk = kc % (KC // 2)
                        self.tr(pt, pt.ap[:, j * 128:(j + 1) * 128], xs, xs.ap[:, kk * 128:(kk + 1) * 128], self.ident)
                    dst = xTv[:, g * 4:(g + 1) * 4, tt * 128:(tt + 1) * 128]
                    src = pt.ap.rearrange("p (k t) -> p k t", k=4)
                    self.cp("dve" if g % 2 == 0 else "act", xT, dst, pt, src)
            na = 0
            for si in range(NSFd):
                ensure(kpos); ensure(kpos + 1)
                w = wl.pop(kpos); kpos += 1
                wv = w.ap.rearrange("p (k j) -> p k j", k=KC)
                for sub in range(4):
                    kind, idx = self.fmb[si * 4 + sub]
                    dst, ddt, fn = self.fm_dest(kind, idx)
                    accs = []
                    for t2 in range(TP // 512):
                        accs.append(acc[na % 6]); na += 1
                    for kc in range(KC):
                        for t2 in range(TP // 512):
                            self.mm(accs[t2], accs[t2].ap, w, wv[:, kc, sub * 128:(sub + 1) * 128],
                                    xT, xTv[:, kc, t2 * 512:(t2 + 1) * 512], start=(kc == 0), stop=(kc == KC - 1))
                    for t2 in range(TP // 512):
                        i = nev[0]; nev[0] += 1
                        ev = (ev32 if ddt == F32 else ev16)[i % 3]
                        if fn is not None:
                            self.act(ev, ev.ap, accs[t2], accs[t2].ap, fn)
                        else:
                            self.cp("act" if i % 2 == 0 else "dve", ev, ev.ap, accs[t2], accs[t2].ap)
                        self.st(dst[:, t0 + t2 * 512:t0 + (t2 + 1) * 512], ev, ev.ap, q="act")
            for sj in range(NSTd):
                ensure(kpos); ensure(kpos + 1)
                w = wl.pop(kpos); kpos += 1
                wv = w.ap.rearrange("p (k j) -> p k j", k=KC)
                half = self.NST // 2
                dst = self.S["Av"] if sj < half else self.S["Dv"]
                c0 = (sj % half) * 512
                for tt in range(TP // 128):
                    a = acc[na % 6]; na += 1
                    for kc in range(KC):
                        self.mm(a, a.ap, xT, xTv[:, kc, tt * 128:(tt + 1) * 128], w, wv[:, kc, :],
                                start=(kc == 0), stop=(kc == KC - 1))
                    i = nev[0]; nev[0] += 1
                    ev = ev16[i % 3]
                    self.cp("act" if i % 2 == 0 else "dve", ev, ev.ap, a, a.ap)
                    self.st(dst[t0 + tt * 128:t0 + (tt + 1) * 128, c0:c0 + 512], ev, ev.ap, q="act")
        self.end()

    def post_setup(self, small=False):
        A = self.A
        n8, n12, n4 = (2, 4, 1) if small else (8, 12, 4)
        self.ptr = self.psum_tiles([6], 64, BF16)
        self.osb = [A.tile(128, F32) for _ in range(n8)]
        self.onb = [A.tile(128, BF16) for _ in range(n8)]
        self.cols = [A.tile(8, F32) for _ in range(n12)]
        self.sqs = [A.tile(128, F32) for _ in range(n4)]
        self.cnt = {"s": 0, "e": 0, "t": 0, "o": 0, "c": 0, "p": 0, "g": 0, "x": 0, "q": 0}

    def attn_setup(self):
        A = self.A
        self.post_setup()
        self.nfill = self.dbg.get("fill", 0)
        if self.nfill:
            self.sbank = self.psum_tiles([0, 1], 512)
            self.fbank = self.psum_tiles([7], 512)[0]
        else:
            self.sbank = self.psum_tiles([0, 1, 7], 512)
        self.oacc = [Sub(self.bk[b], self.banks[b][:, 0:132]) for b in (2, 3, 4, 5)]
        self.E = [A.tile(512, BF16) for _ in range(self.dbg.get('depth', 4) + 2)]
        self.tb = [A.tile(128, F32) for _ in range(3)]

    def nxt(self, lst, key):
        i = self.cnt[key]; self.cnt[key] += 1
        return lst[i % len(lst)]

    def attn_q(self, oacc, Vaug, kbs, smm_fn, scale, bias_fn, done_cb=None):
        Vt, Vv = Vaug
        ngrp = (len(kbs) + 3) // 4
        for gi in range(ngrp):
            grp = kbs[gi * 4:gi * 4 + 4]
            sb = self.nxt(self.sbank, "s")
            for i, kb in enumerate(grp):
                mms = smm_fn(kb)
                for j, (lt, lap, rt, rap) in enumerate(mms):
                    self.mm(sb, sb.ap[:, i * 128:(i + 1) * 128], lt, lap, rt, rap, start=(j == 0), stop=(j == len(mms) - 1))
            pq_ = self.__dict__.setdefault("_pq", [])
            while len(pq_) >= self.dbg.get('depth', 4):
                pq_.pop(0)()
            self.advance_fin()
            E = self.nxt(self.E, "e")
            i = 0
            while i < len(grp):
                b = bias_fn(grp[i])
                if b is None:
                    j = i
                    while j < len(grp) and bias_fn(grp[j]) is None:
                        j += 1
                    self.act(E, E.ap[:, i * 128:j * 128], sb, sb.ap[:, i * 128:j * 128], AF.Exp, scale=scale)
                    i = j
                elif b[0] == "col":
                    self.act(E, E.ap[:, i * 128:(i + 1) * 128], sb, sb.ap[:, i * 128:(i + 1) * 128], AF.Exp,
                             bias=b[2], scale=scale, bt=b[1])
                    i += 1
                else:
                    tb = self.nxt(self.tb, "t")
                    self.stt(tb, tb.ap, sb, sb.ap[:, i * 128:(i + 1) * 128], scale, b[1], b[2], ALU.mult, ALU.add)
                    self.act(E, E.ap[:, i * 128:(i + 1) * 128], tb, tb.ap, AF.Exp)
                    i += 1
            last = (gi == ngrp - 1)

            def pv(grp=grp, E=E, first=(gi == 0), last=last):
                for i, kb in enumerate(grp):
                    self.mm(oacc, oacc.ap[:, 0:129], E, E.ap[:, i * 128:(i + 1) * 128], Vt, Vv[:, kb, 0:129],
                            start=(first and i == 0), stop=(last and i == len(grp) - 1))
                if last and done_cb is not None:
                    done_cb()
            self._pq.append(pv)

    def attn_group(self, oaccs, Vaug, qg, smm_fn, scale, diag_bias, done_cb, diag_sep):
        Vt, Vv = Vaug
        nkb = 4 * qg + 4
        for kb in range(nkb):
            d = kb - 4 * qg if kb >= 4 * qg else None
            qlo = 0 if d is None else d
            sb = self.nxt(self.sbank, "s")
            if diag_sep and d is not None:
                c0w = (d + 1) * 128
            else:
                c0w = qlo * 128
            if c0w < 512:
                mms = smm_fn(kb, c0w, 512, False)
                for j, (lt, lap, rt, rap) in enumerate(mms):
                    self.mm(sb, sb.ap[:, c0w:512], lt, lap, rt, rap, start=(j == 0), stop=(j == len(mms) - 1))
            if diag_sep and d is not None:
                mms = smm_fn(kb, d * 128, (d + 1) * 128, True)
                for j, (lt, lap, rt, rap) in enumerate(mms):
                    self.mm(sb, sb.ap[:, d * 128:(d + 1) * 128], lt, lap, rt, rap, start=(j == 0), stop=(j == len(mms) - 1))
            for _f in range(self.nfill):
                self.P.op("pe", (lambda e, fb=self.fbank, ib=self.identb, ee=self.E[0]:
                                 e.matmul(fb.ap, lhsT=ib.ap, rhs=ee.ap, start=True, stop=True)), writes=[])
            pq_ = self.__dict__.setdefault("_pq", [])
            while len(pq_) >= self.dbg.get('depth', 4):
                pq_.pop(0)()
            self.advance_fin()
            E = self.nxt(self.E, "e")
            e0 = (d + 1) * 128 if d is not None else 0
            if e0 < 512:
                self.act(E, E.ap[:, e0:512], sb, sb.ap[:, e0:512], AF.Exp, scale=scale)
            if d is not None:
                tb = self.nxt(self.tb, "t")
                ds_ = slice(d * 128, (d + 1) * 128)
                self.stt(tb, tb.ap, sb, sb.ap[:, ds_], scale, diag_bias[0], diag_bias[1], ALU.mult, ALU.add)
                self.act(E, E.ap[:, ds_], tb, tb.ap, AF.Exp)

            def pv(kb=kb, E=E, qlo=qlo):
                for qi in range(qlo, 4):
                    last = (kb == 4 * qg + qi)
                    self.mm(oaccs[qi], oaccs[qi].ap[:, 0:129], E, E.ap[:, qi * 128:(qi + 1) * 128], Vt, Vv[:, kb, 0:129],
                            start=(kb == 0), stop=last)
                    if last:
                        done_cb(qi)
            self._pq.append(pv)

    def attn_flush(self):
        pq_ = self.__dict__.setdefault("_pq", [])
        while pq_:
            pq_.pop(0)()
        self.advance_fin(drain=True)

    def finish_o_gen(self, o_t, normw, gate_t, oTst, qb, pre_scale=None):
        cols = self.nxt(self.cols, "c")
        onb = self.nxt(self.onb, "o")
        if pre_scale is not None:
            sq = self.nxt(self.sqs, "q")
            self.tt(sq, sq.ap, o_t, o_t.ap, o_t, o_t.ap, ALU.mult)
            self.rsum(cols, cols.ap[:, 0:1], sq, sq.ap)
            yield
            self.act(cols, cols.ap[:, 1:2], cols, cols.ap[:, 0:1], AF.Ln, bias=self.eps6.ap, scale=1.0 / 128, bt=self.eps6)
            self.act(cols, cols.ap[:, 2:3], cols, cols.ap[:, 1:2], AF.Exp, scale=-0.5)
            yield
            self.ts(onb, onb.ap, o_t, o_t.ap, cols.ap[:, 2:3], ALU.mult, pre_scale, ALU.mult, st=cols)
        else:
            self.cp("dve", onb, onb.ap, o_t, o_t.ap)
        pt = self.nxt(self.ptr, "p")
        self.tr(pt, pt.ap, onb, onb.ap, self.identb)
        yield
        dst = oTst.ap[:, qb * 128:(qb + 1) * 128]
        gap = gate_t.ap[:, qb * 128:(qb + 1) * 128]
        if normw is not None:
            self.stt(oTst, dst, pt, pt.ap, normw.ap, gate_t, gap, ALU.mult, ALU.mult, st=normw)
        else:
            self.tt(oTst, dst, pt, pt.ap, gate_t, gap, ALU.mult)

    def finish_o(self, *a, **k):
        self.__dict__.setdefault("_fin", []).append(self.finish_o_gen(*a, **k))

    def finish_o_now(self, *a, **k):
        for _ in self.finish_o_gen(*a, **k):
            pass

    def advance_fin(self, drain=False):
        fin = self.__dict__.setdefault("_fin", [])
        while True:
            for g_ in list(fin):
                try:
                    next(g_)
                except StopIteration:
                    fin.remove(g_)
            if not drain or not fin:
                break

    def load_vaug(self, Vt, src_tok_major, h):
        Vv = Vt.ap.rearrange("p (b c) -> p b c", c=132)
        src = src_tok_major[:, h * 128:(h + 1) * 128].rearrange("(b p) c -> p b c", p=128)
        self.ld(Vt, Vv[:, :, 0:128], src)
        self.mset(Vt, Vv[:, :, 128:129], 1.0, eng="pool")
        return Vv

    def phaseA(self, l):
        A, NH, T_, NB = self.A, self.NH, self.T, self.NB
        self.begin()
        self.attn_setup()
        QA = [[A.tile(T_, BF16) for _ in range(2)] for _ in range(2)]
        KA = [[A.tile(T_, BF16) for _ in range(2)] for _ in range(2)]
        augs = A.tile(2 * T_, F32)
        Vt = [A.tile(NB * 132, BF16) for _ in range(2)]
        G = [A.tile(T_, BF16) for _ in range(2)]
        oTst = [A.tile(T_, BF16) for _ in range(2)]
        diag = [A.tile(128, F32) for _ in range(2)]
        o0s = [A.tile(128, F32) for _ in range(4)]
        dnorm = A.tile(1, F32)
        lamt = A.tile(256, F32)
        lcol = A.tile(8, F32)
        tmp64 = A.tile(64, F32)
        self.ld(dnorm, dnorm.ap, self.I["dnorm"][l])
        self.ld(lamt, lamt.ap, self.I["dlam"][l].partition_broadcast(128))
        self.tt(tmp64, tmp64.ap, lamt, lamt.ap[:, 0:64], lamt, lamt.ap[:, 64:128], ALU.mult)
        self.rsum(lcol, lcol.ap[:, 0:1], tmp64, tmp64.ap)
        self.tt(tmp64, tmp64.ap, lamt, lamt.ap[:, 128:192], lamt, lamt.ap[:, 192:256], ALU.mult)
        self.rsum(lcol, lcol.ap[:, 1:2], tmp64, tmp64.ap)
        self.act(lcol, lcol.ap[:, 2:4], lcol, lcol.ap[:, 0:2], AF.Exp)
        self.stt(lcol, lcol.ap[:, 4:5], lcol, lcol.ap[:, 3:4], -LAM_INIT[l], lcol, lcol.ap[:, 2:3], ALU.add, ALU.subtract)
        neglam = lcol.ap[:, 4:5]
        Vvs = {}

        def load_head(h):
            b = h % 2
            self.ld(augs, augs.ap[64:67, 0:T_], self.I["c_augK"][h])
            self.ld(augs, augs.ap[64:67, T_:2 * T_], self.I["c_augQ"][h])
            for m_ in range(2):
                self.ld(KA[b][m_], KA[b][m_].ap[0:64, :], self.S["AkT"][h][64 * m_:64 * m_ + 64, :])
                self.ld(QA[b][m_], QA[b][m_].ap[0:64, :], self.S["AqT"][h][64 * m_:64 * m_ + 64, :])
                self.cp("pool", KA[b][m_], KA[b][m_].ap[64:67, :], augs, augs.ap[64:67, 0:T_])
                self.cp("pool", QA[b][m_], QA[b][m_].ap[64:67, :], augs, augs.ap[64:67, T_:2 * T_])
            self.ld(G[b], G[b].ap, self.S["gate"][0 * NH + h])
            self.ld(diag[b], diag[b].ap, self.I["c_diagA"][h])
            Vvs[h] = self.load_vaug(Vt[b], self.S["Av"], h)
        load_head(0)
        for h in range(NH):
            b = h % 2
            if h + 1 < NH:
                load_head(h + 1)
            Vv = Vvs[h]
            for qg in range(NB // 4):
                for m in range(2):
                    ka, qa = KA[b][m], QA[b][m]

                    def smm_fn(kb, c0, c1, diag, ka=ka, qa=qa, qg=qg):
                        nr = 64 if diag else 67
                        return [(ka, ka.ap[0:nr, kb * 128:(kb + 1) * 128], qa, qa.ap[0:nr, qg * 512 + c0:qg * 512 + c1])]

                    def done(qi, m=m, qg=qg, b=b):
                        acc = self.oacc[qi]
                        cols = self.nxt(self.cols, "c")
                        self.recip(cols, cols.ap[:, 0:1], acc, acc.ap[:, 128:129])
                        if m == 0:
                            self.ts(o0s[qi], o0s[qi].ap, acc, acc.ap[:, 0:128], cols.ap[:, 0:1], ALU.mult, st=cols)
                        else:
                            o = self.nxt(self.osb, "o")
                            self.tt(cols, cols.ap[:, 2:3], cols, cols.ap[:, 0:1], lcol, neglam, ALU.mult)
                            self.stt(o, o.ap, acc, acc.ap[:, 0:128], cols.ap[:, 2:3], o0s[qi], o0s[qi].ap, ALU.mult, ALU.add, st=cols)
                            self.finish_o(o, dnorm, G[b], oTst[b], qg * 4 + qi, pre_scale=1.0 - LAM_INIT[l])
                    self.attn_group(self.oacc, (Vt[b], Vv), qg, smm_fn, 0.125, (diag[b], diag[b].ap), done, True)
            self.attn_flush()
            self.st(self.S["oT"][0 * NH + h], oTst[b], oTst[b].ap)
        self.end()

    def phaseD(self, l):
        A, NH, T_, NB = self.A, self.NH, self.T, self.NB
        self.begin()
        self.attn_setup()
        QT = [A.tile(T_, BF16) for _ in range(2)]
        KT = [A.tile(T_, BF16) for _ in range(2)]
        Vt = [A.tile(NB * 132, BF16) for _ in range(2)]
        G = [A.tile(T_, BF16) for _ in range(2)]
        oTst = [A.tile(T_, BF16) for _ in range(2)]
        bias = [A.tile(640, F32) for _ in range(2)]
        mask = A.tile(640, F32)
        self.ld(mask, mask.ap, self.I["c_maskD"])
        scale = 128 ** -0.5
        Vvs = {}

        def load_head(h):
            b = h % 2
            self.ld(QT[b], QT[b].ap, self.S["DqT"][h])
            self.ld(KT[b], KT[b].ap, self.S["DkT"][h])
            self.ld(G[b], G[b].ap, self.S["gate"][3 * NH + h])
            self.ld(bias[b], bias[b].ap, self.I["biasD"][l, h])
            self.tt(bias[b], bias[b].ap, bias[b], bias[b].ap, mask, mask.ap, ALU.add, eng="pool")
            Vvs[h] = self.load_vaug(Vt[b], self.S["Dv"], h)
        load_head(0)
        for h in range(NH):
            b = h % 2
            if h + 1 < NH:
                load_head(h + 1)
            Vv = Vvs[h]
            for qb in range(NB):
                acc = self.nxt(self.oacc, "o")
                kbs = [kb for kb in range(qb - 4, qb + 1) if kb >= 0]
                smm_fn = lambda kb, qb=qb: [(KT[b], KT[b].ap[:, kb * 128:(kb + 1) * 128],
                                             QT[b], QT[b].ap[:, qb * 128:(qb + 1) * 128])]
                bias_fn = lambda kb, qb=qb: ("tile", bias[b], bias[b].ap[:, (qb - kb) * 128:(qb - kb + 1) * 128])
                def done(acc=acc, qb=qb, b=b):
                    cols = self.nxt(self.cols, "c")
                    o = self.nxt(self.osb, "o")
                    self.recip(cols, cols.ap[:, 0:1], acc, acc.ap[:, 128:129])
                    self.ts(o, o.ap, acc, acc.ap[:, 0:128], cols.ap[:, 0:1], ALU.mult, st=cols)
                    self.finish_o(o, None, G[b], oTst[b], qb)
                self.attn_q(acc, (Vt[b], Vv), kbs, smm_fn, scale, bias_fn, done_cb=done)
            self.attn_flush()
            self.st(self.S["oT"][3 * NH + h], oTst[b], oTst[b].ap)
        self.end()

    def phaseC(self, l):
        A, NH, T_, NB = self.A, self.NH, self.T, self.NB
        self.begin()
        self.attn_setup()
        NT5 = T_ // 512
        cqn = A.tile(6 * T_, BF16); cqnv = cqn.ap.rearrange("p (k t) -> p k t", k=6)
        ckvn = A.tile(2 * T_, BF16); ckvnv = ckvn.ap.rearrange("p (k t) -> p k t", k=2)
        kro = A.tile(T_, BF16, parts=64)
        rope = [A.tile(T_, F32, parts=64) for _ in range(2)]
        qn = A.tile(6, F32); kvn = A.tile(2, F32)
        maskC = A.tile(128, F32)
        wuq = [A.tile(6 * 256, BF16) for _ in range(2)]
        wk = [A.tile(2 * 128, BF16) for _ in range(2)]
        wv = A.tile(2 * NH * 128, BF16)
        QnT = A.tile(T_, BF16); QrT = A.tile(T_, BF16, parts=64); KnT = A.tile(T_, BF16)
        Vt = A.tile(NB * 132, BF16)
        G = [A.tile(T_, BF16) for _ in range(2)]
        oTst = [A.tile(T_, BF16) for _ in range(2)]
        stg16 = [A.tile(512, F32) for _ in range(16)]
        nset = [0]
        sqt = [A.tile(512, F32) for _ in range(2)]
        rbc = A.tile(512, F32)
        t64 = [A.tile(512, F32, parts=64) for _ in range(2)]
        big = self.psum_tiles([5], 512)[0]
        oacc4 = [Sub(self.bk[b_], self.banks[b_][:, 0:132]) for b_ in (2, 3, 4, 5)]
        self.ld(qn, qn.ap, self.I["qnorm"][l]); self.ld(kvn, kvn.ap, self.I["kvnorm"][l])
        self.ld(maskC, maskC.ap, self.I["c_maskC"])
        self.ld(rope[0], rope[0].ap, self.I["c_rope"][0]); self.ld(rope[1], rope[1].ap, self.I["c_rope"][1])
        wstg = A.tile(2 * NH * 128, F32)
        self.ld_cast(wv, wv.ap, self.I["wukvv"][l], wstg, "pool")
        wvv = wv.ap.rearrange("p (k c) -> p k c", k=2)
        for (src, nk, nrm, dstv, dim) in ((self.S["Ccq"], 6, qn, cqnv, 768), (self.S["Cckv"], 2, kvn, ckvnv, 256)):
            for t5 in range(NT5):
                ts_ = slice(t5 * 512, (t5 + 1) * 512)
                chs = [stg16[(nset[0] % 2) * 8 + k] for k in range(nk)]
                nset[0] += 1
                for k in range(nk):
                    s = chs[k]
                    self.ld(s, s.ap, src[k][:, ts_])
                for k in range(nk):
                    s = chs[k]
                    q = sqt[k % 2]
                    self.tt(q, q.ap, s, s.ap, s, s.ap, ALU.mult)
                    self.mm(big, big.ap, self.ones, self.ones.ap, q, q.ap, start=(k == 0), stop=(k == nk - 1))
                self.act(rbc, rbc.ap, big, big.ap, AF.Sqrt, bias=self.eps6.ap, scale=1.0 / dim, bt=self.eps6)
                self.recip(rbc, rbc.ap, rbc, rbc.ap)
                for k in range(nk):
                    s = chs[k]
                    self.stt((cqn if nk == 6 else ckvn), dstv[:, k, ts_], s, s.ap, nrm.ap[:, k:k + 1], rbc, rbc.ap,
                             ALU.mult, ALU.mult, st=nrm)
        for t5 in range(NT5):
            ts_ = slice(t5 * 512, (t5 + 1) * 512)
            a, b2 = t64[0], t64[1]
            self.ld(a, a.ap, self.S["Ckr"][0][0:64, ts_])
            self.ld(b2, b2.ap, self.S["Ckr"][1][0:64, ts_])
            self.tt(a, a.ap, a, a.ap, rope[0], rope[0].ap[:, ts_], ALU.mult)
            self.tt(b2, b2.ap, b2, b2.ap, rope[1], rope[1].ap[:, ts_], ALU.mult)
            self.tt(kro, kro.ap[:, ts_], a, a.ap, b2, b2.ap, ALU.add)
        scale = 192 ** -0.5
        pq = big
        def load_head(h):
            b = h % 2
            self.ld_cast(wuq[b], wuq[b].ap, self.I["wuq"][l, h], wstg, "pool")
            self.ld_cast(wk[b], wk[b].ap, self.I["wukvk"][l, h], wstg, "pool")
            self.ld(G[b], G[b].ap, self.S["gate"][2 * NH + h])
        load_head(0)
        for h in range(NH):
            b = h % 2
            if h + 1 < NH:
                load_head(h + 1)
            wq = wuq[b].ap.rearrange("p (k c) -> p k c", k=6)
            wkk = wk[b].ap.rearrange("p (k c) -> p k c", k=2)
            for t5 in range(NT5):
                ts_ = slice(t5 * 512, (t5 + 1) * 512)
                for k in range(6):
                    self.mm(pq, pq.ap, wuq[b], wq[:, k, 0:128], cqn, cqnv[:, k, ts_], start=(k == 0), stop=(k == 5))
                self.cp("act", QnT, QnT.ap[:, ts_], pq, pq.ap)
                a, b2 = t64[0], t64[1]
                for k in range(6):
                    self.mm(pq, pq.ap[0:64, :], wuq[b], wq[:, k, 128:192], cqn, cqnv[:, k, ts_], start=(k == 0), stop=(k == 5))
                self.tt(a, a.ap, pq, pq.ap[0:64, :], rope[0], rope[0].ap[:, ts_], ALU.mult)
                for k in range(6):
                    self.mm(pq, pq.ap[0:64, :], wuq[b], wq[:, k, 192:256], cqn, cqnv[:, k, ts_], start=(k == 0), stop=(k == 5))
                self.tt(b2, b2.ap, pq, pq.ap[0:64, :], rope[1], rope[1].ap[:, ts_], ALU.mult)
                self.tt(QrT, QrT.ap[:, ts_], a, a.ap, b2, b2.ap, ALU.add)
                for k in range(2):
                    self.mm(pq, pq.ap, wk[b], wkk[:, k, :], ckvn, ckvnv[:, k, ts_], start=(k == 0), stop=(k == 1))
                self.cp("act", KnT, KnT.ap[:, ts_], pq, pq.ap)
            Vv = Vt.ap.rearrange("p (b c) -> p b c", c=132)
            self.mset(Vt, Vv[:, :, 128:129], 1.0, eng="pool")
            for tb_ in range(NB):
                pv = self.nxt(self.sbank, "s")
                for k in range(2):
                    self.mm(pv, pv.ap[:, 0:128], ckvn, ckvnv[:, k, tb_ * 128:(tb_ + 1) * 128], wv, wvv[:, k, h * 128:(h + 1) * 128],
                            start=(k == 0), stop=(k == 1))
                self.cp("dve", Vt, Vv[:, tb_, 0:128], pv, pv.ap[:, 0:128])
            for qg in range(NB // 4):
                def smm_fn(kb, c0, c1, diag, qg=qg):
                    ks = slice(kb * 128, (kb + 1) * 128); qs = slice(qg * 512 + c0, qg * 512 + c1)
                    return [(KnT, KnT.ap[:, ks], QnT, QnT.ap[:, qs]), (kro, kro.ap[:, ks], QrT, QrT.ap[:, qs])]

                def done(qi, qg=qg, b=b):
                    acc = oacc4[qi]
                    cols = self.nxt(self.cols, "c")
                    o = self.nxt(self.osb, "o")
                    self.recip(cols, cols.ap[:, 0:1], acc, acc.ap[:, 128:129])
                    self.ts(o, o.ap, acc, acc.ap[:, 0:128], cols.ap[:, 0:1], ALU.mult, st=cols)
                    self.finish_o(o, None, G[b], oTst[b], qg * 4 + qi)
                self.attn_group(oacc4, (Vt, Vv), qg, smm_fn, scale, (maskC, maskC.ap), done, False)
            self.attn_flush()
            self.st(self.S["oT"][2 * NH + h], oTst[b], oTst[b].ap)
        self.end()


    def phaseB(self, l):
        A, NH, T_, NB = self.A, self.NH, self.T, self.NB
        self.begin()
        self.post_setup(small=True)
        mmr = lambda *a, **k: self.mm(*a, r=True, **k)
        NT5 = T_ // 512
        big = lambda: A.tile(T_, F32)
        t_q, t_k, t_v, tmp1, tmp2 = big(), big(), big(), big(), big()
        gbc, bbc, egbc, kbT, kbgT, qgT = big(), big(), big(), big(), big(), big()
        kdT = bbc
        Gt = A.tile(T_, F32, parts=NH); nGt = A.tile(T_, F32, parts=NH); beta = A.tile(T_, F32, parts=NH)
        sel = A.tile(NH * 128, F32, parts=NH)
        tri = A.tile(3 * 128, F32)
        gnorm = A.tile(1, F32)
        gcols = A.tile(4, F32, parts=NH)
        cw = A.tile(12, F32)
        eglast = A.tile(NB, F32)
        XL = [A.tile(256, F32) for _ in range(NB)]
        qkL = [A.tile(128, F32) for _ in range(NB)]
        wL = [A.tile(128, F32) for _ in range(NB)]
        kdL = [A.tile(128, F32) for _ in range(NB)]
        NBATCH = 4
        Mbb = [[[A.tile(128, F32) for _ in range(NBATCH)] for _ in range(2)] for _ in range(2)]
        MTbb = [[[A.tile(128, F32) for _ in range(NBATCH)] for _ in range(2)] for _ in range(2)]
        DTbb = [[A.tile(128, F32) for _ in range(NBATCH)] for _ in range(2)]
        Sst = A.tile(128, F32)
        vnew = [A.tile(128, F32) for _ in range(2)]
        Gg = A.tile(T_, BF16); oTst = A.tile(T_, BF16)
        gtl = [self.psum_tiles([0, 1], 128), self.psum_tiles([4, 5], 128)]
        xtl = [self.psum_tiles([2, 3], 256), self.psum_tiles([6, 7], 256)]
        bcp = self.psum_tiles([4, 5], 512)
        sc4 = [Sub(self.bk[b_], self.banks[b_][:, 0:128]) for b_ in (0, 1, 2, 3)]
        self.ld(sel, sel.ap, self.I["c_sel"])
        self.ld(tri, tri.ap.rearrange("p (k c) -> p k c", k=3), self.I["c_tri"].rearrange("k p c -> p k c"))
        U_incl, U_strict = tri.ap[:, 0:128], tri.ap[:, 128:256]
        self.ld(gnorm, gnorm.ap, self.I["gnorm"][l])
        self.ld(gcols, gcols.ap[:, 0:1], self.I["alog"][l]); self.ld(gcols, gcols.ap[:, 1:2], self.I["dtb"][l])
        a_t = T(tmp1.ap[0:NH, :]); c_t = T(tmp2.ap[0:NH, :])
        self.ld(a_t, a_t.ap, self.S["Bab"][0][0:NH, :])
        self.ld(beta, beta.ap, self.S["Bab"][1][0:NH, :])
        self.act(beta, beta.ap, beta, beta.ap, AF.Sigmoid)
        self.act(a_t, a_t.ap, a_t, a_t.ap, AF.Exp, bias=gcols.ap[:, 1:2], bt=gcols)
        self.act(a_t, a_t.ap, a_t, a_t.ap, AF.Ln, bias=1.0)
        self.act(gcols, gcols.ap[:, 2:3], gcols, gcols.ap[:, 0:1], AF.Exp)
        self.ts(gcols, gcols.ap[:, 3:4], gcols, gcols.ap[:, 2:3], -1.0, ALU.mult)
        self.ts(a_t, a_t.ap, a_t, a_t.ap, gcols.ap[:, 3:4], ALU.mult, st=gcols)
        src, dst = a_t, c_t
        sh = 1
        while sh < 128:
            sv = src.ap.rearrange("p (b c) -> p b c", c=128); dv = dst.ap.rearrange("p (b c) -> p b c", c=128)
            self.cp("pool", dst, dv[:, :, 0:sh], src, sv[:, :, 0:sh])
            self.tt(dst, dv[:, :, sh:128], src, sv[:, :, sh:128], src, sv[:, :, 0:128 - sh], ALU.add)
            src, dst = dst, src
            sh *= 2
        self.cp("dve", Gt, Gt.ap, src, src.ap)
        self.ts(nGt, nGt.ap, src, src.ap, -1.0, ALU.mult)
        self.P.barrier()
        qgTs = [qgT, A.tile(T_, F32)]
        eglasts = [eglast, A.tile(NB, F32)]

        def prologue_gen(h):
            selh = sel.ap[:, h * 128:(h + 1) * 128]
            qgT = qgTs[h % 2]; eglast = eglasts[h % 2]
            for ty, dstt in enumerate((t_q, t_k, t_v)):
                self.ld(tmp1, tmp1.ap, self.S["Bpre"][ty * NH + h])
                self.ld(cw, cw.ap[:, ty * 4:ty * 4 + 4], self.I["convw"][l, ty * NH + h])
                w = lambda j: cw.ap[:, ty * 4 + j:ty * 4 + j + 1]
                self.ts(tmp2, tmp2.ap, tmp1, tmp1.ap, w(3), ALU.mult, st=cw)
                yield
                for j in range(3):
                    s_ = 3 - j
                    self.stt(tmp2, tmp2.ap[:, s_:T_], tmp1, tmp1.ap[:, 0:T_ - s_], w(j), tmp2, tmp2.ap[:, s_:T_],
                             ALU.mult, ALU.add, st=cw)
                    yield
                self.act(dstt, dstt.ap, tmp2, tmp2.ap, AF.Silu)
                yield
            for tq, extra in ((t_q, 128 ** -0.5), (t_k, 1.0)):
                self.tt(tmp1, tmp1.ap, tq, tq.ap, tq, tq.ap, ALU.mult)
                yield
                for t5 in range(NT5):
                    ts_ = slice(t5 * 512, (t5 + 1) * 512)
                    ps = bcp[t5 % 2]
                    mmr(ps, ps.ap, self.ones, self.ones.ap, tmp1, tmp1.ap[:, ts_])
                    self.act(tmp2, tmp2.ap[:, ts_], ps, ps.ap, AF.Sqrt, bias=self.eps6.ap, bt=self.eps6)
                    yield
                self.recip(tmp2, tmp2.ap, tmp2, tmp2.ap)
                yield
                self.stt(tq, tq.ap, tq, tq.ap, extra, tmp2, tmp2.ap, ALU.mult, ALU.mult)
                yield
            for (srcg, dstb) in ((Gt, gbc), (beta, bbc)):
                for t5 in range(NT5):
                    ts_ = slice(t5 * 512, (t5 + 1) * 512)
                    ps = bcp[t5 % 2]
                    mmr(ps, ps.ap, sel, selh, srcg, srcg.ap[:, ts_])
                    self.cp("act" if t5 % 2 else "dve", dstb, dstb.ap[:, ts_], ps, ps.ap)
                    yield
            self.act(egbc, egbc.ap, gbc, gbc.ap, AF.Exp)
            self.tt(kbT, kbT.ap, t_k, t_k.ap, bbc, bbc.ap, ALU.mult)
            yield
            self.tt(kbgT, kbgT.ap, kbT, kbT.ap, egbc, egbc.ap, ALU.mult, eng="pool")
            self.tt(qgT, qgT.ap, t_q, t_q.ap, egbc, egbc.ap, ALU.mult)
            yield
            self.tt(t_v, t_v.ap, t_v, t_v.ap, bbc, bbc.ap, ALU.mult, eng="pool")
            gv = gbc.ap.rearrange("p (b c) -> p b c", c=128)
            self.act(eglast, eglast.ap, gbc, gv[:, :, 127], AF.Exp)
            yield
            for j in range(NB):
                bs = slice(j * 128, (j + 1) * 128)
                self.act(tmp1, tmp1.ap[:, bs], gbc, gbc.ap[:, bs], AF.Exp, bias=gbc.ap[:, j * 128 + 127:j * 128 + 128], scale=-1.0, bt=gbc)
                if j % 4 == 3:
                    yield
            self.tt(kdT, kdT.ap, t_k, t_k.ap, tmp1, tmp1.ap, ALU.mult)
            yield

        for _ in prologue_gen(0):
            pass
        for h in range(NH):
            selh = sel.ap[:, h * 128:(h + 1) * 128]
            qgT = qgTs[h % 2]; eglast = eglasts[h % 2]
            self.ld(Gg, Gg.ap, self.S["gate"][1 * NH + h])
            def batch_gen(js, gt, xt_, Mb, MTb, DTb):
                cnt = {"g": 0, "x": 0}

                def ng():
                    cnt["g"] += 1
                    return gt[(cnt["g"] - 1) % len(gt)]

                def nx():
                    cnt["x"] += 1
                    return xt_[(cnt["x"] - 1) % len(xt_)]
                cur = 0
                pss = {}
                for i, j in enumerate(js):
                    bs = slice(j * 128, (j + 1) * 128)
                    p = ng(); pss[i] = p
                    mmr(p, p.ap, sel, selh, Gt, Gt.ap[:, bs], start=True, stop=False)
                    mmr(p, p.ap, nGt, nGt.ap[:, bs], sel, selh, start=False, stop=True)
                yield
                for i, j in enumerate(js):
                    self.ts(DTb[i], DTb[i].ap, pss[i], pss[i].ap, 0.0, ALU.min)
                    self.act(DTb[i], DTb[i].ap, DTb[i], DTb[i].ap, AF.Exp)
                yield
                for i, j in enumerate(js):
                    bs = slice(j * 128, (j + 1) * 128)
                    p = ng(); pss[i] = p
                    mmr(p, p.ap, t_k, t_k.ap[:, bs], kbT, kbT.ap[:, bs])
                yield
                for i, j in enumerate(js):
                    LT = MTb[cur][i]
                    self.tt(LT, LT.ap, pss[i], pss[i].ap, DTb[i], DTb[i].ap, ALU.mult)
                    self.tt(LT, LT.ap, LT, LT.ap, tri, U_strict, ALU.mult, eng="pool")
                yield
                for i, j in enumerate(js):
                    bs = slice(j * 128, (j + 1) * 128)
                    p = ng(); pss[i] = p
                    mmr(p, p.ap, t_k, t_k.ap[:, bs], t_q, t_q.ap[:, bs])
                yield
                for i, j in enumerate(js):
                    self.tt(qkL[j], qkL[j].ap, pss[i], pss[i].ap, DTb[i], DTb[i].ap, ALU.mult)
                    self.tt(qkL[j], qkL[j].ap, qkL[j], qkL[j].ap, tri, U_incl, ALU.mult, eng="pool")
                yield
                for i, j in enumerate(js):
                    p = ng(); pss[i] = p
                    self.tr(p, p.ap, MTb[cur][i], MTb[cur][i].ap, self.ident)
                yield
                for i, j in enumerate(js):
                    self.cp("act", Mb[cur][i], Mb[cur][i].ap, pss[i], pss[i].ap)
                yield
                xps = {}
                for i, j in enumerate(js):
                    bs = slice(j * 128, (j + 1) * 128)
                    p = nx(); xps[i] = p
                    self.tr(p, p.ap[:, 0:128], t_v, t_v.ap[:, bs], self.ident)
                    self.tr(p, p.ap[:, 128:256], kbgT, kbgT.ap[:, bs], self.ident)
                yield
                for i, j in enumerate(js):
                    self.cp("dve", XL[j], XL[j].ap, xps[i], xps[i].ap)
                yield
                for i, j in enumerate(js):
                    bs = slice(j * 128, (j + 1) * 128)
                    p = ng(); pss[i] = p
                    self.tr(p, p.ap, kdT, kdT.ap[:, bs], self.ident)
                yield
                for i, j in enumerate(js):
                    self.cp("act", kdL[j], kdL[j].ap, pss[i], pss[i].ap)
                yield
                for lev in range(7):
                    if lev > 0:
                        nxt_ = 1 - cur
                        p2 = {}
                        for i, j in enumerate(js):
                            p = ng(); pss[i] = p
                            mmr(p, p.ap, Mb[cur][i], Mb[cur][i].ap, MTb[cur][i], MTb[cur][i].ap)
                            if lev < 6:
                                q_ = ng(); p2[i] = q_
                                mmr(q_, q_.ap, MTb[cur][i], MTb[cur][i].ap, Mb[cur][i], Mb[cur][i].ap)
                        yield
                        for i, j in enumerate(js):
                            self.cp("act", MTb[nxt_][i], MTb[nxt_][i].ap, pss[i], pss[i].ap)
                            if lev < 6:
                                self.cp("act" if (lev + i) % 2 else "dve", Mb[nxt_][i], Mb[nxt_][i].ap, p2[i], p2[i].ap)
                        yield
                        cur = nxt_
                    for i, j in enumerate(js):
                        p = nx(); xps[i] = p
                        mmr(p, p.ap, MTb[cur][i], MTb[cur][i].ap, XL[j], XL[j].ap)
                    yield
                    for i, j in enumerate(js):
                        self.tt(XL[j], XL[j].ap, XL[j], XL[j].ap, xps[i], xps[i].ap,
                                ALU.subtract if lev == 0 else ALU.add)
                    yield
                for i, j in enumerate(js):
                    p = ng(); pss[i] = p
                    self.tr(p, p.ap, XL[j], XL[j].ap[:, 128:256], self.ident)
                yield
                for i, j in enumerate(js):
                    self.cp("act", wL[j], wL[j].ap, pss[i], pss[i].ap)
                yield

            for j0 in range(0, NB, 2 * NBATCH):
                gens = []
                for bi in range(2):
                    js = list(range(j0 + bi * NBATCH, min(NB, j0 + (bi + 1) * NBATCH)))
                    if js:
                        gens.append(batch_gen(js, gtl[bi], xtl[bi], Mbb[bi], MTbb[bi], DTbb[bi]))
                while gens:
                    for g_ in list(gens):
                        try:
                            next(g_)
                        except StopIteration:
                            gens.remove(g_)
            npg = prologue_gen(h + 1) if h + 1 < NH else iter(())
            self.mset(Sst, Sst.ap, 0.0)
            for j in range(NB):
                for _ in range(3):
                    next(npg, None)
                bs = slice(j * 128, (j + 1) * 128)
                vn = vnew[j % 2]
                p1 = sc4[(3 * j) % 4]; po = sc4[(3 * j + 1) % 4]; pS = sc4[(3 * j + 2) % 4]
                mmr(p1, p1.ap, wL[j], wL[j].ap, Sst, Sst.ap)
                self.tt(vn, vn.ap, XL[j], XL[j].ap[:, 0:128], p1, p1.ap, ALU.subtract)
                mmr(po, po.ap, qgT, qgT.ap[:, bs], Sst, Sst.ap, start=True, stop=False)
                mmr(po, po.ap, qkL[j], qkL[j].ap, vn, vn.ap, start=False, stop=True)
                mmr(pS, pS.ap, kdL[j], kdL[j].ap, vn, vn.ap)
                self.stt(Sst, Sst.ap, Sst, Sst.ap, eglast.ap[:, j:j + 1], pS, pS.ap, ALU.mult, ALU.add, st=eglast)
                o = self.nxt(self.osb, "o")
                self.cp("act", o, o.ap, po, po.ap)
                self.finish_o_now(o, gnorm, Gg, oTst, j, pre_scale=1.0)
            for _ in npg:
                pass
            self.st(self.S["oT"][1 * NH + h], oTst, oTst.ap)
        self.end()

    def phase3(self, l):
        A, NH, T_, TP = self.A, self.NH, self.T, self.TP
        self.begin()
        NK = 4 * NH
        oT = A.tile(NK * TP, BF16); oTv = oT.ap.rearrange("p (k t) -> p k t", k=NK)
        wb = [A.tile(NK * 512, BF16) for _ in range(2)]
        wst = [A.tile(NK * 512 // 8, F32) for _ in range(4)]
        ev = [A.tile(512, F32) for _ in range(4)]
        acc = self.psum_tiles(list(range(8)), 512)
        ne = 0
        nstg = 0
        seq3 = [(p, db) for p in range(T_ // TP) for db in range(8)]
        wl3 = {}

        def ensure3(k):
            nonlocal nstg
            if k < len(seq3) and k not in wl3:
                _, db_ = seq3[k]
                w_ = wb[k % 2]
                src = self.I["wout"][l, db_]
                q = NK * 512 // 8
                for i in range(8):
                    self.ld_cast(w_, w_.ap[:, i * q:(i + 1) * q], src[:, i * q:(i + 1) * q], wst[nstg % 4],
                                 "dve" if nstg % 2 == 0 else "act")
                    nstg += 1
                wl3[k] = w_
        k3 = 0
        for p in range(T_ // TP):
            t0 = p * TP
            for k in range(NK):
                self.ld(oT, oTv[:, k, :], self.S["oT"][k][:, t0:t0 + TP])
            for db in range(8):
                ensure3(k3); ensure3(k3 + 1)
                w = wl3.pop(k3); k3 += 1
                wv = w.ap.rearrange("p (k j) -> p k j", k=NK)
                ntt = TP // 128
                for k in range(NK):
                    for tt in range(ntt):
                        self.mm(acc[tt], acc[tt].ap, oT, oTv[:, k, tt * 128:(tt + 1) * 128], w, wv[:, k, :],
                                start=(k == 0), stop=(k == NK - 1))
                for tt in range(ntt):
                    e = ev[ne % 4]; ne += 1
                    self.cp("act" if ne % 2 == 0 else "dve", e, e.ap, acc[tt], acc[tt].ap)
                    self.st(self.S["y"][t0 + tt * 128:t0 + (tt + 1) * 128, db * 512:(db + 1) * 512], e, e.ap, q="act")
        self.end()

    def phaseN(self, l, xin, xout, final):
        A, T_ = self.A, self.T
        self.begin()
        gb = A.tile(D, F32); bb = A.tile(D, F32)
        self.ld(gb, gb.ap, self.I["lng"][l].partition_broadcast(128))
        self.ld(bb, bb.ap, self.I["lnb"][l].partition_broadcast(128))
        xt = [A.tile(D, F32) for _ in range(2)]
        yt = [A.tile(D, F32) for _ in range(2)]
        zt = [A.tile(D, F32) for _ in range(2)]
        cols = [A.tile(8, F32) for _ in range(2)]
        def load_tile(tt):
            b = tt % 2
            rows = slice(tt * 128, (tt + 1) * 128)
            self.ld(xt[b], xt[b].ap, xin[rows, :])
            self.ld(yt[b], yt[b].ap, self.S["y"][rows, :])
        load_tile(0)
        for tt in range(T_ // 128):
            b = tt % 2
            rows = slice(tt * 128, (tt + 1) * 128)
            x_, y_, z_, c_ = xt[b], yt[b], zt[b], cols[b]
            if tt + 1 < T_ // 128:
                load_tile(tt + 1)
            self.stt(z_, z_.ap, x_, x_.ap, ALPHA, y_, y_.ap, ALU.mult, ALU.add)
            self.rsum(c_, c_.ap[:, 0:1], z_, z_.ap)
            self.act(y_, y_.ap, z_, z_.ap, AF.Square)
            self.rsum(c_, c_.ap[:, 2:3], y_, y_.ap)
            self.ts(c_, c_.ap[:, 1:2], c_, c_.ap[:, 0:1], 1.0 / D, ALU.mult)
            self.tt(c_, c_.ap[:, 3:4], c_, c_.ap[:, 1:2], c_, c_.ap[:, 1:2], ALU.mult)
            self.stt(c_, c_.ap[:, 5:6], c_, c_.ap[:, 2:3], 1.0 / D, c_, c_.ap[:, 3:4], ALU.mult, ALU.subtract)
            self.act(c_, c_.ap[:, 6:7], c_, c_.ap[:, 5:6], AF.Ln, bias=self.eps5.ap, bt=self.eps5)
            self.act(c_, c_.ap[:, 4:5], c_, c_.ap[:, 6:7], AF.Exp, scale=-0.5)
            self.stt(z_, z_.ap, z_, z_.ap, c_.ap[:, 1:2], gb, gb.ap, ALU.subtract, ALU.mult, st=c_)
            self.stt(z_, z_.ap, z_, z_.ap, c_.ap[:, 4:5], bb, bb.ap, ALU.mult, ALU.add, st=c_)
            self.st(xout[rows, :], z_, z_.ap, final=final)
        self.end()


def _cols(NH, hs):
    ar = np.arange(128)
    ar64 = np.arange(64)
    blocks = []
    for kind, idx in fm_blocks(NH):
        if kind == "Aq": c = O_AQ + hs[idx] * 128 + ar
        elif kind == "Ak": c = O_AK + hs[idx] * 128 + ar
        elif kind == "Bq": c = O_BQ + hs[idx] * 128 + ar
        elif kind == "Bk": c = O_BK + hs[idx] * 128 + ar
        elif kind == "Bv": c = O_BV + hs[idx] * 128 + ar
        elif kind == "Ccq": c = O_CQ + idx * 128 + ar
        elif kind == "Cckv": c = O_CKV + idx * 128 + ar
        elif kind == "Ckr":
            r = ar64 if idx == 0 else (ar64 + 32) % 64
            c = O_CKR + np.concatenate([r, r])
        elif kind == "Bab":
            base = O_BA if idx == 0 else O_BB
            c = np.array([base + h for h in hs] + [base + hs[0]] * (128 - NH))
        elif kind == "Dq": c = O_DQ + hs[idx] * 128 + ar
        elif kind == "Dk": c = O_DK + hs[idx] * 128 + ar
        elif kind == "G":
            m, hh = divmod(idx, NH)
            c = O_G + m * 1024 + hs[hh] * 128 + ar
        blocks.append(c)
    av = np.concatenate([O_AV + h * 128 + ar for h in hs])
    dv = np.concatenate([O_DV + h * 128 + ar for h in hs])
    return np.concatenate(blocks + [av, dv])


def prep_shared(inp, NH, hs, T_, L):
    f32 = np.float32
    out = {}
    inp = {k: (v if k == "x" else v[:L]) for k, v in inp.items()}
    cols = _cols(NH, hs)
    ns1 = len(cols) // 512
    w1 = np.empty((L, ns1, 128, KC * 512), f32)
    for l in range(L):
        wsel = inp["w_in"][l][:, cols]
        w1[l] = wsel.reshape(KC, 128, ns1, 512).transpose(2, 1, 0, 3).reshape(ns1, 128, KC * 512)
    out["w1"] = w1
    NK = 4 * NH
    mixrows = np.concatenate([m * 1024 + h * 128 + np.arange(128) for m in range(4) for h in hs])
    wout = np.empty((L, 8, 128, NK * 512), f32)
    for l in range(L):
        ws = inp["w_out"][l][mixrows]
        wout[l] = ws.reshape(NK, 128, 8, 512).transpose(2, 1, 0, 3).reshape(8, 128, NK * 512)
    out["wout"] = wout
    wuq = np.empty((L, NH, 128, 6, 256), f32)
    wk = np.empty((L, NH, 128, 2, 128), f32)
    wv = np.empty((L, 128, 2, NH, 128), f32)
    sw = (np.arange(64) + 32) % 64
    for l in range(L):
        uq = inp["mla_w_uq"][l].reshape(6, 128, 8, 192)
        ukv = inp["mla_w_ukv"][l].reshape(2, 128, 8, 256)
        for i, h in enumerate(hs):
            wuq[l, i, :, :, 0:128] = uq[:, :, h, 0:128].transpose(1, 0, 2)
            wuq[l, i, :, :, 128:192] = uq[:, :, h, 128:192].transpose(1, 0, 2)
            wuq[l, i, :, :, 192:256] = uq[:, :, h, 128:192][:, :, sw].transpose(1, 0, 2)
            wk[l, i] = ukv[:, :, h, 0:128].transpose(1, 0, 2)
            wv[l, :, :, i, :] = ukv[:, :, h, 128:256].transpose(1, 0, 2)
    out["wuq"] = wuq.reshape(L, NH, 128, 6 * 256)
    out["wukvk"] = wk.reshape(L, NH, 128, 256)
    out["wukvv"] = wv.reshape(L, 128, 2 * NH * 128)
    out["qnorm"] = np.ascontiguousarray(inp["mla_q_norm"].reshape(L, 6, 128).transpose(0, 2, 1))
    out["kvnorm"] = np.ascontiguousarray(inp["mla_kv_norm"].reshape(L, 2, 128).transpose(0, 2, 1))
    out["dnorm"] = np.ascontiguousarray(inp["diff_norm"].reshape(L, 128, 1))
    out["gnorm"] = np.ascontiguousarray(inp["gdn_norm"].reshape(L, 128, 1))
    out["dlam"] = np.ascontiguousarray(inp["diff_lambda"].reshape(L, 1, 256))
    out["alog"] = np.ascontiguousarray(inp["gdn_a_log"][:, hs].reshape(L, NH, 1))
    out["dtb"] = np.ascontiguousarray(inp["gdn_dt_bias"][:, hs].reshape(L, NH, 1))
    cw = np.empty((L, 3 * NH, 128, 4), f32)
    for l in range(L):
        g = inp["gdn_conv"][l].reshape(4, 3, 8, 128)
        for ty in range(3):
            for i, h in enumerate(hs):
                cw[l, ty * NH + i] = g[:, ty, h, :].T
    out["convw"] = cw
    ki = np.arange(128)[:, None]
    qi = np.arange(128)[None, :]
    bd = np.empty((L, NH, 128, 5, 128), f32)
    md = np.empty((128, 5, 128), f32)
    for dl in range(5):
        idx = np.clip(128 * dl + qi - ki, -128, 128) + 128
        for l in range(L):
            for i, h in enumerate(hs):
                bd[l, i, :, dl, :] = inp["rel_bias"][l, h][idx]
        cd = 2 * dl + (qi >= 64).astype(int) - (ki >= 64).astype(int)
        md[:, dl, :] = np.where((cd >= 0) & (cd <= 8), 0.0, NEG)
    out["biasD"] = bd.reshape(L, NH, 128, 640)
    out["c_maskD"] = md.reshape(128, 640)
    out["lng"] = np.ascontiguousarray(inp["ln_gain"].reshape(L, 1, D))
    out["lnb"] = np.ascontiguousarray(inp["ln_bias"].reshape(L, 1, D))
    slopes = 2.0 ** (-8.0 * np.arange(1, 9) / 8.0)
    augK = np.empty((NH, 3, T_), f32)
    augQ = np.empty((NH, 3, T_), f32)
    tpos = np.arange(T_)
    dg = np.empty((NH, 128, 128), f32)
    vis = (ki // 64) <= (qi // 64)
    for i, h in enumerate(hs):
        sl = slopes[h]
        augK[i] = np.stack([8.0 * sl * (tpos % 128), 1024.0 * sl * (tpos // 128), np.ones(T_)])
        augQ[i] = np.stack([np.ones(T_), np.ones(T_), -1024.0 * sl * (tpos // 128)])
        dg[i] = np.where(vis, -sl * np.abs(qi - ki) + sl * qi, NEG)
    out["c_augK"] = augK
    out["c_augQ"] = augQ
    out["c_diagA"] = dg
    out["c_maskC"] = np.where(vis, 0.0, NEG).astype(f32)
    inv = 10000.0 ** (-np.arange(32, dtype=np.float64) / 32.0)
    ang = (np.arange(T_, dtype=np.float64)[None, :] * inv.astype(np.float32).astype(np.float64)[:, None])
    ang = np.concatenate([ang, ang], axis=0)
    sgn = np.concatenate([-np.ones(32), np.ones(32)])[:, None]
    out["c_rope"] = np.stack([np.cos(ang), sgn * np.sin(ang)]).astype(f32)
    sel = np.zeros((NH, NH, 128), f32)
    for i in range(NH):
        sel[i, i, :] = 1.0
    out["c_sel"] = sel.reshape(NH, NH * 128)
    s_ = np.arange(128)[:, None]; c_ = np.arange(128)[None, :]
    out["c_tri"] = np.stack([(c_ >= s_), (c_ > s_), (c_ < s_)]).astype(f32)
    out["c_ident"] = np.eye(128, dtype=f32)
    return out
from concourse.bass_utils import run_bass_kernel_spmd

NH_CFG = 8
T_CFG = 2048
L_CFG = 2


def build_nc(NH, T_, L, dbg=None, arena_kib=200):
    nc = bass.Bass("TRN2", target_bir_lowering=False)
    b = Builder(nc, NH, T_, L, dbg=dbg, arena_kib=arena_kib)
    b.build()
    return nc, b


def kernel(**inputs):
    inp = {k: np.asarray(v) for k, v in inputs.items()}
    NH, T_, L = NH_CFG, T_CFG, L_CFG
    hs = list(range(8))
    shared = prep_shared(inp, NH, hs, T_, L)
    nc, _ = build_nc(NH, T_, L)
    in_maps = []
    for c in range(8):
        m = dict(shared)
        m["x"] = np.ascontiguousarray(inp["x"][c % 4])
        in_maps.append(m)
    res = run_bass_kernel_spmd(nc, in_maps, core_ids=list(range(8)))
    out = np.stack([np.asarray(res.results[b]["out"]) for b in range(4)]).astype(np.float32)
    return out
```
